# Optimizing a Trainium2 kernel written in Bass

```python
import jax, jax.numpy as jnp
from jax import lax
import numpy as np

D_MODEL = 1024
BATCH = 2
SEQ = 16384
DEPTH = 4

GRID_W = 64
CTX_LEN = 256
N_MIXERS = 3
D_FF = 4 * D_MODEL
EPS = 1e-6

GLA_HEADS = 4
GLA_KEY_DIM = D_MODEL // 2
GLA_VAL_DIM = D_MODEL
GLA_DK = GLA_KEY_DIM // GLA_HEADS
GLA_DV = GLA_VAL_DIM // GLA_HEADS
GLA_RANK = 16
GLA_TAU = 16.0
GLA_CHUNK = 64

CONV_WIDTH = 31
CONV_PAD = CONV_WIDTH // 2

ATTN_HEAD_DIM = 64
ATTN_Q_HEADS = D_MODEL // ATTN_HEAD_DIM
ATTN_KV_HEADS = 4
ATTN_GROUP = ATTN_Q_HEADS // ATTN_KV_HEADS
ATTN_Q_DIM = ATTN_Q_HEADS * ATTN_HEAD_DIM
ATTN_KV_DIM = ATTN_KV_HEADS * ATTN_HEAD_DIM
ATTN_BLOCK = 128
ATTN_SCALE = ATTN_HEAD_DIM ** -0.5
ROPE_THETA = 10000.0

N_GLA = (DEPTH + 2) // 3
N_CONV = (DEPTH + 1) // 3
N_ATTN = DEPTH // 3

kernel_name = "hybrid_gla_conformer_gqa_dit_trunk"


def rms_norm(x, g):
    xf = x.astype(jnp.float32)
    y = xf * lax.rsqrt(jnp.mean(xf * xf, axis=-1, keepdims=True) + EPS)
    return (y * g.astype(jnp.float32)).astype(x.dtype)


def layer_norm(x, g, b):
    xf = x.astype(jnp.float32)
    mu = jnp.mean(xf, axis=-1, keepdims=True)
    xc = xf - mu
    y = xc * lax.rsqrt(jnp.mean(xc * xc, axis=-1, keepdims=True) + EPS)
    return (y * g.astype(jnp.float32) + b.astype(jnp.float32)).astype(x.dtype)


def split_heads(t, n):
    B, L, _ = t.shape
    return t.reshape(B, L, n, -1).transpose(0, 2, 1, 3)


def axial_rope_tables(row_idx, col_idx):
    n = ATTN_HEAD_DIM // 4
    inv = ROPE_THETA ** (-jnp.arange(n, dtype=jnp.float32) / n)
    ang = jnp.concatenate([row_idx.astype(jnp.float32)[:, None] * inv,
                           col_idx.astype(jnp.float32)[:, None] * inv], axis=-1)
    return jnp.cos(ang), jnp.sin(ang)


def apply_rope(x, cos, sin):
    half = x.shape[-1] // 2
    cos = cos.astype(x.dtype)
    sin = sin.astype(x.dtype)
    x1, x2 = x[..., :half], x[..., half:]
    return jnp.concatenate([x1 * cos - x2 * sin, x2 * cos + x1 * sin], axis=-1)


def gla_chunk_scan(q, k, v, logg, s0):
    B, H, L, _ = q.shape
    n = L // GLA_CHUNK

    def to_chunks(a):
        return jnp.moveaxis(a.reshape(B, H, n, GLA_CHUNK, a.shape[-1]), 2, 0)

    mask = jnp.tril(jnp.ones((GLA_CHUNK, GLA_CHUNK), dtype=bool))

    def step(s, inp):
        qc, kc, vc, gc = inp
        b = jnp.cumsum(gc, axis=-2)
        b_last = b[..., -1:, :]
        qb = qc * jnp.exp(b)
        kb = kc * jnp.exp(-b)
        a = jnp.where(mask, jnp.einsum('bhid,bhjd->bhij', qb, kb), 0.0)
        o = jnp.einsum('bhid,bhde->bhie', qb, s) + jnp.einsum('bhij,bhje->bhie', a, vc)
        kd = kc * jnp.exp(b_last - b)
        s_new = jnp.exp(b_last[..., 0, :])[..., None] * s + jnp.einsum('bhjd,bhje->bhde', kd, vc)
        return s_new, o

    s_fin, o = lax.scan(step, s0, (to_chunks(q), to_chunks(k), to_chunks(v), to_chunks(logg)))
    o = jnp.moveaxis(o, 0, 2).reshape(B, H, L, v.shape[-1])
    return o, s_fin


def gla_mixer(h_ctx, h_lat, w_in, w_g1, w_g2, b_g, g_head, w_out, need_ctx):
    f32 = jnp.float32

    def project(h):
        z = h @ w_in
        q, k, v, r = jnp.split(z, [GLA_KEY_DIM, 2 * GLA_KEY_DIM, 2 * GLA_KEY_DIM + GLA_VAL_DIM], axis=-1)
        q = split_heads(q, GLA_HEADS).astype(f32) * (GLA_DK ** -0.5)
        k = split_heads(k, GLA_HEADS).astype(f32)
        v = split_heads(v, GLA_HEADS).astype(f32)
        gates = []
        for d in range(2):
            zg = (h @ w_g1[d]) @ w_g2[d] + b_g[d]
            gates.append(split_heads(jax.nn.log_sigmoid(zg.astype(f32)) / GLA_TAU, GLA_HEADS))
        return q, k, v, r, gates[0], gates[1]

    qc, kc, vc, rc, gcf, gcb = project(h_ctx)
    ql, kl, vl, rl, glf, glb = project(h_lat)

    def flip(a):
        return jnp.flip(a, axis=2)

    B = h_lat.shape[0]
    s0 = jnp.zeros((B, GLA_HEADS, GLA_DK, GLA_DV), f32)
    oc_f, s_f = gla_chunk_scan(qc, kc, vc, gcf, s0)
    oc_b, s_b = gla_chunk_scan(flip(qc), flip(kc), flip(vc), flip(gcb), s0)
    ol_f, _ = gla_chunk_scan(ql, kl, vl, glf, s_f)
    ol_b, _ = gla_chunk_scan(flip(ql), flip(kl), flip(vl), flip(glb), s_b)

    def finish(o, r, dtype):
        o = rms_norm(o, g_head)
        o = o.transpose(0, 2, 1, 3).reshape(o.shape[0], o.shape[2], GLA_VAL_DIM).astype(dtype)
        return (o * jax.nn.silu(r)) @ w_out

    y_lat = finish(ol_f + flip(ol_b), rl, h_lat.dtype)
    y_ctx = finish(oc_f + flip(oc_b), rc, h_ctx.dtype) if need_ctx else None
    return y_ctx, y_lat


def conformer_conv(h, w_pw1, b_pw1, w_dw, b_dw, ln_g, ln_b, w_pw2, b_pw2):
    z = h @ w_pw1 + b_pw1
    a, gt = jnp.split(z, 2, axis=-1)
    u = a * jax.nn.sigmoid(gt)
    u = lax.conv_general_dilated(u, w_dw[:, None, :], window_strides=(1,),
                                 padding=[(CONV_PAD, CONV_PAD)],
                                 dimension_numbers=('NWC', 'WIO', 'NWC'),
                                 feature_group_count=D_MODEL) + b_dw
    u = jax.nn.silu(layer_norm(u, ln_g, ln_b))
    return u @ w_pw2 + b_pw2


def sdpa(q, k, v):
    s = jnp.einsum('bkgqd,bksd->bkgqs', q, k).astype(jnp.float32) * ATTN_SCALE
    p = jax.nn.softmax(s, axis=-1).astype(v.dtype)
    return jnp.einsum('bkgqs,bksd->bkgqd', p, v)


def gqa_mixer(h_ctx, h_lat, w_qkv, g_q, g_k, w_out, cos, sin, need_ctx):
    def project(h, rope):
        B, L, _ = h.shape
        z = h @ w_qkv
        q, k, v = jnp.split(z, [ATTN_Q_DIM, ATTN_Q_DIM + ATTN_KV_DIM], axis=-1)
        q = rms_norm(q.reshape(B, L, ATTN_KV_HEADS, ATTN_GROUP, ATTN_HEAD_DIM), g_q).transpose(0, 2, 3, 1, 4)
        k = rms_norm(k.reshape(B, L, ATTN_KV_HEADS, ATTN_HEAD_DIM), g_k).transpose(0, 2, 1, 3)
        v = v.reshape(B, L, ATTN_KV_HEADS, ATTN_HEAD_DIM).transpose(0, 2, 1, 3)
        if rope:
            q = apply_rope(q, cos, sin)
            k = apply_rope(k, cos, sin)
        return q, k, v

    def merge(o):
        B, _, _, L, _ = o.shape
        return o.transpose(0, 3, 1, 2, 4).reshape(B, L, ATTN_Q_DIM) @ w_out

    qc, kc, vc = project(h_ctx, False)
    ql, kl, vl = project(h_lat, True)
    y_ctx = merge(sdpa(qc, kc, vc)) if need_ctx else None

    k_all = jnp.concatenate([kc, kl], axis=2)
    v_all = jnp.concatenate([vc, vl], axis=2)
    B, KV, G, L, hd = ql.shape
    nb = L // ATTN_BLOCK
    qb = jnp.moveaxis(ql.reshape(B, KV, G, nb, ATTN_BLOCK, hd), 3, 0)
    ob = lax.map(lambda qq: sdpa(qq, k_all, v_all), qb)
    ol = jnp.moveaxis(ob, 0, 3).reshape(B, KV, G, L, hd)
    return y_ctx, merge(ol)


def sq_relu_mlp(h, w_in, w_out):
    return jnp.square(jax.nn.relu(h @ w_in)) @ w_out


def setup_inputs(seed: int = 0) -> dict:
    key = jax.random.key(seed)
    ks = iter(jax.random.split(key, 40))
    D = D_MODEL

    def nrm(shape, scale):
        return jax.random.normal(next(ks), shape, jnp.float32) * scale

    return {
        "x": nrm((BATCH, SEQ, D), 1.0),
        "c": nrm((BATCH, D), 1.0),
        "ctx": nrm((BATCH, CTX_LEN, D), 1.0),
        "c_ctx": nrm((D,), 1.0),
        "w_mod": nrm((DEPTH, D, 6 * D), 0.5 * D ** -0.5),
        "b_mod": nrm((DEPTH, 6 * D), 0.01),
        "g_norm_mix": 1.0 + nrm((DEPTH, D), 0.02),
        "g_norm_mlp": 1.0 + nrm((DEPTH, D), 0.02),
        "w_mlp_in": nrm((DEPTH, D, D_FF), D ** -0.5),
        "w_mlp_out": nrm((DEPTH, D_FF, D), D_FF ** -0.5),
        "gla_w_in": nrm((N_GLA, D, 2 * GLA_KEY_DIM + 2 * GLA_VAL_DIM), D ** -0.5),
        "gla_w_g1": nrm((N_GLA, 2, D, GLA_RANK), D ** -0.5),
        "gla_w_g2": nrm((N_GLA, 2, GLA_RANK, GLA_KEY_DIM), GLA_RANK ** -0.5),
        "gla_b_g": nrm((N_GLA, 2, GLA_KEY_DIM), 0.1),
        "gla_g_head": 1.0 + nrm((N_GLA, GLA_DV), 0.02),
        "gla_w_out": nrm((N_GLA, GLA_VAL_DIM, D), GLA_VAL_DIM ** -0.5),
        "conv_w_pw1": nrm((N_CONV, D, 2 * D), D ** -0.5),
        "conv_b_pw1": nrm((N_CONV, 2 * D), 0.01),
        "conv_w_dw": nrm((N_CONV, CONV_WIDTH, D), CONV_WIDTH ** -0.5),
        "conv_b_dw": nrm((N_CONV, D), 0.01),
        "conv_ln_g": 1.0 + nrm((N_CONV, D), 0.02),
        "conv_ln_b": nrm((N_CONV, D), 0.01),
        "conv_w_pw2": nrm((N_CONV, D, D), D ** -0.5),
        "conv_b_pw2": nrm((N_CONV, D), 0.01),
        "attn_w_qkv": nrm((N_ATTN, D, ATTN_Q_DIM + 2 * ATTN_KV_DIM), D ** -0.5),
        "attn_g_q": 1.0 + nrm((N_ATTN, ATTN_HEAD_DIM), 0.02),
        "attn_g_k": 1.0 + nrm((N_ATTN, ATTN_HEAD_DIM), 0.02),
        "attn_w_out": nrm((N_ATTN, ATTN_Q_DIM, D), ATTN_Q_DIM ** -0.5),
    }


def reference(x, c, ctx, c_ctx, w_mod, b_mod, g_norm_mix, g_norm_mlp, w_mlp_in, w_mlp_out,
              gla_w_in, gla_w_g1, gla_w_g2, gla_b_g, gla_g_head, gla_w_out,
              conv_w_pw1, conv_b_pw1, conv_w_dw, conv_b_dw, conv_ln_g, conv_ln_b, conv_w_pw2, conv_b_pw2,
              attn_w_qkv, attn_g_q, attn_g_k, attn_w_out):
    L = x.shape[1]
    rows = L // GRID_W
    row_idx = jnp.repeat(jnp.arange(rows, dtype=jnp.int32), GRID_W)
    col_idx = jnp.tile(jnp.arange(GRID_W, dtype=jnp.int32), rows)
    cos, sin = axial_rope_tables(row_idx, col_idx)

    silu_c = jax.nn.silu(c)
    silu_cc = jax.nn.silu(c_ctx)
    x_lat, x_ctx = x, ctx
    for i in range(DEPTH):
        last = i == DEPTH - 1
        kind = i % N_MIXERS
        j = i // N_MIXERS
        mod_l = (silu_c @ w_mod[i] + b_mod[i])[:, None, :]
        mod_c = (silu_cc @ w_mod[i] + b_mod[i])[None, None, :]
        sh1_l, sc1_l, gt1_l, sh2_l, sc2_l, gt2_l = jnp.split(mod_l, 6, axis=-1)
        sh1_c, sc1_c, gt1_c, sh2_c, sc2_c, gt2_c = jnp.split(mod_c, 6, axis=-1)

        h_l = rms_norm(x_lat, g_norm_mix[i]) * (1.0 + sc1_l) + sh1_l
        h_c = rms_norm(x_ctx, g_norm_mix[i]) * (1.0 + sc1_c) + sh1_c
        if kind == 0:
            y_c, y_l = gla_mixer(h_c, h_l, gla_w_in[j], gla_w_g1[j], gla_w_g2[j], gla_b_g[j],
                                 gla_g_head[j], gla_w_out[j], not last)
        elif kind == 1:
            conv_args = (conv_w_pw1[j], conv_b_pw1[j], conv_w_dw[j], conv_b_dw[j],
                         conv_ln_g[j], conv_ln_b[j], conv_w_pw2[j], conv_b_pw2[j])
            y_l = conformer_conv(h_l, *conv_args)
            y_c = None if last else conformer_conv(h_c, *conv_args)
        else:
            y_c, y_l = gqa_mixer(h_c, h_l, attn_w_qkv[j], attn_g_q[j], attn_g_k[j], attn_w_out[j],
                                 cos, sin, not last)
        x_lat = x_lat + gt1_l * y_l

        h_l = rms_norm(x_lat, g_norm_mlp[i]) * (1.0 + sc2_l) + sh2_l
        x_lat = x_lat + gt2_l * sq_relu_mlp(h_l, w_mlp_in[i], w_mlp_out[i])
        if not last:
            x_ctx = x_ctx + gt1_c * y_c
            h_c = rms_norm(x_ctx, g_norm_mlp[i]) * (1.0 + sc2_c) + sh2_c
            x_ctx = x_ctx + gt2_c * sq_relu_mlp(h_c, w_mlp_in[i], w_mlp_out[i])
    return x_lat
```

```python
import contextlib
import numpy as np
import concourse.bass as bass
import concourse.mybir as mybir
from concourse.bass_utils import run_bass_kernel_spmd

F32 = mybir.dt.float32
BF16 = mybir.dt.bfloat16
AF = mybir.ActivationFunctionType
ALU = mybir.AluOpType
AX = mybir.AxisListType


class Buf:
    __slots__ = ("name", "last_w", "readers")

    def __init__(self, name):
        self.name = name
        self.last_w = None
        self.readers = []


class Op:
    __slots__ = ("eng", "fn", "deps", "dma_key", "sig", "val", "sem")

    def __init__(self, eng, fn, dma_key=None):
        self.eng = eng
        self.fn = fn
        self.deps = []
        self.dma_key = dma_key
        self.sig = False
        self.val = 0
        self.sem = None


ENGS = ("pe", "act", "dve", "pool", "sp")
SEM_EPOCH = 30000


class Prog:
    def __init__(self, nc):
        self.nc = nc
        self.q = {e: [] for e in ENGS}
        self.dma_cnt = {}
        self.stack = contextlib.ExitStack()
        self.nbuf = 0

    def sbuf(self, name, shape, dt):
        return self.stack.enter_context(self.nc.sbuf_tensor(name, list(shape), dt))

    def psum(self, name, shape, dt=F32):
        return self.stack.enter_context(self.nc.psum_tensor(name, list(shape), dt))

    def buf(self, name=None):
        self.nbuf += 1
        return Buf(name or f"b{self.nbuf}")

    def op(self, eng, fn, r=(), w=(), dma_key=None):
        o = Op(eng, fn, dma_key)
        deps = []
        for b in r:
            if b.last_w is not None:
                deps.append(b.last_w)
        for b in w:
            if b.last_w is not None:
                deps.append(b.last_w)
            deps.extend(b.readers)
        seen = set()
        for d in deps:
            if id(d) in seen or d is o:
                continue
            seen.add(id(d))
            if d.dma_key is None and d.eng == eng and eng in ("pe", "sp"):
                continue
            o.deps.append(d)
            if d.dma_key is None:
                d.sig = True
        for b in r:
            b.readers.append(o)
        for b in w:
            b.last_w = o
            b.readers = []
        if dma_key is not None:
            c = self.dma_cnt.get(dma_key, 0) + 16
            self.dma_cnt[dma_key] = c
            o.val = c
        self.q[eng].append(o)
        return o

    def dma(self, out, in_, r=(), w=(), key=None, eng="sp"):
        assert key is not None
        return self.op(eng, lambda e: e.dma_start(out=out, in_=in_), r=r, w=w, dma_key=key)

    def wait_all(self, eng, bufs):
        return self.op(eng, None, r=bufs)

    def emit(self):
        nc = self.nc
        st = self.stack
        eng_sems = {}
        for e in ENGS:
            n = 0
            for o in self.q[e]:
                if o.sig and o.dma_key is None:
                    ep = n // SEM_EPOCH
                    key = (e, ep)
                    if key not in eng_sems:
                        eng_sems[key] = st.enter_context(nc.semaphore(f"s_{e}{ep}"))
                    o.sem = eng_sems[key]
                    o.val = n % SEM_EPOCH + 1
                    n += 1
        dma_sems = {}
        for k in self.dma_cnt:
            dma_sems[k] = st.enter_context(nc.semaphore(f"d_{k}"))
        for e in ENGS:
            for o in self.q[e]:
                if o.dma_key is not None:
                    o.sem = dma_sems[o.dma_key]
        self.n_sems = len(eng_sems) + len(dma_sems)

        def run(ename, eng):
            waited = {}
            for o in self.q[ename]:
                need = {}
                for d in o.deps:
                    sid = id(d.sem)
                    if waited.get(sid, 0) >= d.val and not (d.dma_key is None and False):
                        continue
                    if sid not in need or need[sid][1] < d.val:
                        need[sid] = (d.sem, d.val)
                for sid, (sem, val) in need.items():
                    eng.wait_ge(sem, val)
                    waited[sid] = val
                if o.fn is None:
                    continue
                ins = o.fn(eng)
                if o.dma_key is not None:
                    ins.then_inc(o.sem, 16)
                elif o.sig:
                    ins.then_inc(o.sem, 1)

        with nc.Block() as block:
            if self.q["pe"]:
                @block.tensor
                def _(e):
                    run("pe", e)
            if self.q["act"]:
                @block.scalar
                def _(e):
                    run("act", e)
            if self.q["dve"]:
                @block.vector
                def _(e):
                    run("dve", e)
            if self.q["pool"]:
                @block.gpsimd
                def _(e):
                    run("pool", e)
            if self.q["sp"]:
                @block.sync
                def _(e):
                    run("sp", e)

    def close(self):
        self.stack.close()


D = 1024
KC = 8


class Common:
    def __init__(self, p, nmax=512):
        self.p = p
        self.nmax = nmax
        self.ones = p.sbuf("ones_bf", [128, 128], BF16)
        self.b_ones = p.buf("ones")
        self.eps = p.sbuf("eps_t", [128, 1], F32)
        self.b_eps = p.buf("eps")
        p.op("dve", lambda e: e.memset(self.ones[:], 1.0), w=[self.b_ones])
        p.op("dve", lambda e: e.memset(self.eps[:], 1e-6), w=[self.b_eps])
        self.sq = [p.sbuf(f"sq{i}", [128, nmax], BF16) for i in range(2)]
        self.b_sq = [p.buf(f"sq{i}") for i in range(2)]
        self.tmp = [p.sbuf(f"nt{i}", [128, nmax], F32) for i in range(2)]
        self.b_tmp = [p.buf(f"nt{i}") for i in range(2)]
        self.ssum = p.psum("ssum", [128, nmax])
        self.b_ssum = p.buf("ssum")
        self.rstd = p.sbuf("rstd", [128, nmax], F32)
        self.b_rstd = p.buf("rstd")
        self.cnt = 0

    def load_vec(self, name, dram_ap, shape):
        t = self.p.sbuf(name + "_sb", shape, F32)
        b = self.p.buf(name)
        self.p.dma(t[:], dram_ap, w=[b], key=name)
        return t, b

    def make_AB(self, name, g, bg, gi, mod, bmod, sc_i, sh_i):
        p = self.p
        A = p.sbuf(name + "_A", [128, KC], F32)
        bA = p.buf(name + "_A")
        p.op("dve", lambda e: e.scalar_tensor_tensor(out=A[:], in0=mod[:, sc_i, :], scalar=1.0, in1=g[:, gi, :],
                                                      op0=ALU.add, op1=ALU.mult), r=[bg, bmod], w=[bA])
        return A, bA

    def norm_mod(self, xs, bxs, n, A, bA, Bm, bB, sh_i, h, bh):
        p = self
        P = self.p
        for k in range(KC):
            i = self.cnt % 2
            self.cnt += 1
            sq, bsq = self.sq[i], self.b_sq[i]
            P.op("act", lambda e, sq=sq, k=k: e.activation(out=sq[:, :n], in_=xs[k], func=AF.Square),
                 r=[bxs[k]], w=[bsq])
            P.op("pe", lambda e, sq=sq, k=k: e.matmul(self.ssum[:, :n], lhsT=self.ones[:], rhs=sq[:, :n],
                                                      start=(k == 0), stop=(k == KC - 1)),
                 r=[bsq, self.b_ones], w=[self.b_ssum])
        P.op("act", lambda e: e.activation(out=self.rstd[:, :n], in_=self.ssum[:, :n], func=AF.Sqrt,
                                           bias=self.eps[:, 0:1], scale=1.0 / D),
             r=[self.b_ssum, self.b_eps], w=[self.b_rstd])
        P.op("dve", lambda e: e.reciprocal(self.rstd[:, :n], self.rstd[:, :n]), r=[self.b_rstd], w=[self.b_rstd])
        for k in range(KC):
            i = self.cnt % 2
            self.cnt += 1
            t, bt = self.tmp[i], self.b_tmp[i]
            P.op("dve", lambda e, t=t, k=k: e.tensor_tensor(out=t[:, :n], in0=xs[k], in1=self.rstd[:, :n], op=ALU.mult),
                 r=[bxs[k], self.b_rstd], w=[bt])
            P.op("act", lambda e, t=t, k=k: e.activation(out=h[:, k, :n], in_=t[:, :n], func=AF.Identity,
                                                        bias=Bm[:, sh_i, k:k + 1], scale=A[:, k:k + 1]),
                 r=[bt, bA, bB], w=[bh])


def build_mlp(T=4096, TC=256, NB=512):
    FF = 4096
    MC = FF // 128
    nc = bass.Bass("TRN2", target_bir_lowering=False)
    xT = nc.dram_tensor("xT", [D, T], F32, kind="ExternalInput").ap()
    cT = nc.dram_tensor("cT", [D, TC], F32, kind="ExternalInput").ap() if TC else None
    w1 = nc.dram_tensor("w1", [D, FF], BF16, kind="ExternalInput").ap()
    w2 = nc.dram_tensor("w2", [FF, D], BF16, kind="ExternalInput").ap()
    gn = nc.dram_tensor("gn", [128, 1, KC], F32, kind="ExternalInput").ap()
    modl = nc.dram_tensor("modl", [128, 3, KC], F32, kind="ExternalInput").ap()
    modc = nc.dram_tensor("modc", [128, 3, KC], F32, kind="ExternalInput").ap()
    oT = nc.dram_tensor("oT", [D, T], F32, kind="ExternalOutput").ap()
    ocT = nc.dram_tensor("ocT", [D, TC], F32, kind="ExternalOutput").ap() if TC else None
    p = Prog(nc)
    cm = Common(p, NB)
    W1 = p.sbuf("W1", [128, KC, FF], BF16)
    W2 = p.sbuf("W2", [128, MC, D], BF16)
    bW1 = [p.buf(f"W1_{k}") for k in range(KC)]
    bW2 = [p.buf(f"W2_{j}") for j in range(4)]
    for k in range(KC):
        p.dma(W1[:, k, :], w1[k * 128:(k + 1) * 128, :], w=[bW1[k]], key=f"W1_{k}")
    w2v = w2.rearrange("(m p) d -> p m d", p=128)
    for j in range(4):
        p.dma(W2[:, j * 8:(j + 1) * 8, :], w2v[:, j * 8:(j + 1) * 8, :], w=[bW2[j]], key=f"W2_{j}")
    g, bg = cm.load_vec("gn", gn, [128, 1, KC])
    ml, bml = cm.load_vec("modl", modl, [128, 3, KC])
    mc_, bmc = cm.load_vec("modc", modc, [128, 3, KC])
    Al, bAl = cm.make_AB("l", g, bg, 0, ml, bml, 1, 0)
    Ac, bAc = cm.make_AB("c", g, bg, 0, mc_, bmc, 1, 0)

    xin = [p.sbuf(f"xin{k}", [128, NB], F32) for k in range(KC)]
    bxin = [p.buf(f"xin{k}") for k in range(KC)]
    h = p.sbuf("h", [128, KC, NB], BF16)
    bh = p.buf("h")
    gbuf = p.sbuf("g", [128, MC, NB], BF16)
    bg_m = [p.buf(f"g{m}") for m in range(MC)]
    rb = [p.sbuf(f"rb{i}", [128, NB], BF16) for i in range(2)]
    brb = [p.buf(f"rb{i}") for i in range(2)]
    ob = [p.sbuf(f"ob{i}", [128, NB], F32) for i in range(2)]
    bob = [p.buf(f"ob{i}") for i in range(2)]
    NPS = 3
    ps = [p.psum(f"ps{i}", [128, NB]) for i in range(NPS)]
    bps = [p.buf(f"ps{i}") for i in range(NPS)]
    bout = p.buf("out")
    cnt = {"ps": 0, "rb": 0, "ob": 0}

    def block(src, dst, t0, n, A, bA, mod, bmod):
        xs = [xin[k][:, :n] for k in range(KC)]
        for k in range(KC):
            p.dma(xin[k][:, :n], src[k * 128:(k + 1) * 128, t0:t0 + n], w=[bxin[k]], key=f"xin{k}")
        cm.norm_mod(xs, bxin, n, A, bA, mod, bmod, 0, h, bh)
        for m in range(MC):
            i = cnt["ps"] % NPS
            cnt["ps"] += 1
            for k in range(KC):
                p.op("pe", lambda e, i=i, m=m, k=k: e.matmul(ps[i][:, :n], lhsT=W1[:, k, m * 128:(m + 1) * 128],
                                                             rhs=h[:, k, :n], start=(k == 0), stop=(k == KC - 1)),
                     r=[bW1[k], bh], w=[bps[i]])
            j = cnt["rb"] % 2
            cnt["rb"] += 1
            p.op("act", lambda e, i=i, j=j: e.activation(out=rb[j][:, :n], in_=ps[i][:, :n], func=AF.Relu),
                 r=[bps[i]], w=[brb[j]])
            p.op("dve", lambda e, i=i, j=j, m=m: e.scalar_tensor_tensor(out=gbuf[:, m, :n], in0=ps[i][:, :n], scalar=0.0,
                                                                         in1=rb[j][:, :n], op0=ALU.max, op1=ALU.mult),
                 r=[bps[i], brb[j]], w=[bg_m[m]])
        for c in range(KC):
            i = cnt["ps"] % NPS
            cnt["ps"] += 1
            for m in range(MC):
                p.op("pe", lambda e, i=i, m=m, c=c: e.matmul(ps[i][:, :n], lhsT=W2[:, m, c * 128:(c + 1) * 128],
                                                             rhs=gbuf[:, m, :n], start=(m == 0), stop=(m == MC - 1)),
                     r=[bW2[m // 8], bg_m[m]], w=[bps[i]])
            j = cnt["ob"] % 2
            cnt["ob"] += 1
            p.op("dve", lambda e, i=i, j=j, c=c: e.scalar_tensor_tensor(out=ob[j][:, :n], in0=ps[i][:, :n],
                                                                         scalar=mod[:, 2, c:c + 1], in1=xin[c][:, :n],
                                                                         op0=ALU.mult, op1=ALU.add),
                 r=[bps[i], bmod, bxin[c]], w=[bob[j]])
            p.dma(dst[c * 128:(c + 1) * 128, t0:t0 + n], ob[j][:, :n], r=[bob[j]], w=[bout], key=f"ob{j}")

    if TC:
        block(cT, ocT, 0, TC, Ac, bAc, mc_, bmc)
    for b in range(T // NB):
        block(xT, oT, b * NB, NB, Al, bAl, ml, bml)
    p.wait_all("sp", bob)
    p.emit()
    p.close()
    return nc


HALO = 15
CW = 31


def build_conv(T=4096, TC=256, NB=256):
    nc = bass.Bass("TRN2", target_bir_lowering=False)
    xe = nc.dram_tensor("xe", [D, T + 2 * HALO], F32, kind="ExternalInput").ap()
    ce = nc.dram_tensor("ce", [D, TC + 2 * HALO], F32, kind="ExternalInput").ap()
    wp1 = nc.dram_tensor("wp1", [D, 2 * D], BF16, kind="ExternalInput").ap()
    wp2 = nc.dram_tensor("wp2", [D, D], BF16, kind="ExternalInput").ap()
    ident = nc.dram_tensor("ident", [128, 128], F32, kind="ExternalInput").ap()
    vecs = nc.dram_tensor("vecs", [128, 7, KC], F32, kind="ExternalInput").ap()
    wdw = nc.dram_tensor("wdw", [128, KC, CW], F32, kind="ExternalInput").ap()
    modl = nc.dram_tensor("modl", [128, 3, KC], F32, kind="ExternalInput").ap()
    modc = nc.dram_tensor("modc", [128, 3, KC], F32, kind="ExternalInput").ap()
    hmask = nc.dram_tensor("hmask", [128, 2], F32, kind="ExternalInput").ap()
    oT = nc.dram_tensor("oT", [D, T], F32, kind="ExternalOutput").ap()
    ocT = nc.dram_tensor("ocT", [D, TC], F32, kind="ExternalOutput").ap()
    p = Prog(nc)
    cm = Common(p, NB)
    W1 = p.sbuf("W1", [128, KC, 2 * D], BF16)
    W2 = p.sbuf("W2", [128, KC, D], BF16)
    bW1 = [p.buf(f"W1_{k}") for k in range(KC)]
    bW2 = p.buf("W2")
    for k in range(KC):
        p.dma(W1[:, k, :], wp1[k * 128:(k + 1) * 128, :], w=[bW1[k]], key=f"W1_{k}")
    p.dma(W2[:], wp2.rearrange("(k p) d -> p k d", p=128), w=[bW2], key="W2")
    V, bV = cm.load_vec("vecs", vecs, [128, 7, KC])
    ml, bml = cm.load_vec("modl", modl, [128, 3, KC])
    mc_, bmc = cm.load_vec("modc", modc, [128, 3, KC])
    hm, bhm = cm.load_vec("hmask", hmask, [128, 2])
    wd, bwd = cm.load_vec("wdw", wdw, [128, KC, CW])
    idf, bidf = cm.load_vec("ident", ident, [128, 128])
    Al, bAl = cm.make_AB("l", V, bV, 0, ml, bml, 1, 0)
    Ac, bAc = cm.make_AB("c", V, bV, 0, mc_, bmc, 1, 0)
    zero2 = p.sbuf("zero2", [128, 2], F32)
    bz2 = p.buf("zero2")
    p.op("dve", lambda e: e.memset(zero2[:], 0.0), w=[bz2])
    dg = p.sbuf("dg", [128, KC * CW, 128], BF16)
    bdg = p.buf("dg")
    for c in range(KC):
        for tp in range(CW):
            p.op("dve",
                 lambda e, c=c, tp=tp: e.tensor_scalar(out=dg[:, c * CW + tp, :], in0=idf[:], scalar1=wd[:, c, tp:tp + 1],
                                                       scalar2=None, op0=ALU.mult),
                 r=[bidf, bwd], w=[bdg])
    ones_f = p.sbuf("ones_f", [128, 128], F32)
    bones_f = p.buf("ones_f")
    p.op("dve", lambda e: e.memset(ones_f[:], 1.0), w=[bones_f])
    bgl = p.sbuf("bgl", [128, KC], F32)
    bbgl = p.buf("bgl")
    bgc = p.sbuf("bgc", [128, KC], F32)
    bbgc = p.buf("bgc")
    p.op("dve", lambda e: e.tensor_tensor(out=bgl[:], in0=V[:, 6, :], in1=ml[:, 2, :], op=ALU.mult), r=[bV, bml], w=[bbgl])
    p.op("dve", lambda e: e.tensor_tensor(out=bgc[:], in0=V[:, 6, :], in1=mc_[:, 2, :], op=ALU.mult), r=[bV, bmc], w=[bbgc])

    UW = NB + 2 * HALO
    U = [p.sbuf(f"U{i}", [128, KC, UW], BF16) for i in range(3)]
    bU = [p.buf(f"U{i}") for i in range(3)]
    xin = [p.sbuf(f"xin{k}", [128, NB], F32) for k in range(KC)]
    bxin = [p.buf(f"xin{k}") for k in range(KC)]
    xr = [p.sbuf(f"xr{i}", [128, NB], F32) for i in range(2)]
    bxr = [p.buf(f"xr{i}") for i in range(2)]
    h = p.sbuf("h", [128, KC, NB], BF16)
    bh = p.buf("h")
    v = p.sbuf("v", [128, KC, NB], F32)
    bv = [p.buf(f"v{c}") for c in range(KC)]
    v2 = [p.sbuf(f"v2_{i}", [128, NB], F32) for i in range(2)]
    bv2 = [p.buf(f"v2_{i}") for i in range(2)]
    sg = [p.sbuf(f"sg{i}", [128, NB], F32) for i in range(2)]
    bsg = [p.buf(f"sg{i}") for i in range(2)]
    s = p.sbuf("s", [128, KC, NB], BF16)
    bs = p.buf("s")
    ob = [p.sbuf(f"ob{i}", [128, NB], F32) for i in range(2)]
    bob = [p.buf(f"ob{i}") for i in range(2)]
    mean = p.sbuf("mean", [128, NB], F32)
    bmean = p.buf("mean")
    lr = p.sbuf("lr", [128, NB], F32)
    blr = p.buf("lr")
    t1 = [p.sbuf(f"t1_{i}", [128, NB], F32) for i in range(2)]
    bt1 = [p.buf(f"t1_{i}") for i in range(2)]
    NPS = 3
    ps = [p.psum(f"ps{i}", [128, NB]) for i in range(NPS)]
    bps = [p.buf(f"ps{i}") for i in range(NPS)]
    psm = p.psum("psm", [128, NB])
    bpsm = p.buf("psm")
    psq = p.psum("psq", [128, NB])
    bpsq = p.buf("psq")
    bout = p.buf("out")
    cnt = {"ps": 0, "sg": 0, "ob": 0, "v2": 0, "t1": 0, "xr": 0}

    def nxt(name, n):
        i = cnt[name] % n
        cnt[name] += 1
        return i

    def stageA(src, e0, n, ub, A, bA, mod, bmod):
        xs = [xin[k][:, :n] for k in range(KC)]
        for k in range(KC):
            p.dma(xin[k][:, :n], src[k * 128:(k + 1) * 128, e0:e0 + n], w=[bxin[k]], key=f"xin{k}")
        cm.norm_mod(xs, bxin, n, A, bA, mod, bmod, 0, h, bh)
        for c in range(KC):
            ia = nxt("ps", NPS)
            for k in range(KC):
                p.op("pe", lambda e, ia=ia, c=c, k=k: e.matmul(ps[ia][:, :n], lhsT=W1[:, k, c * 128:(c + 1) * 128],
                                                               rhs=h[:, k, :n], start=(k == 0), stop=(k == KC - 1)),
                     r=[bW1[k], bh], w=[bps[ia]])
            ig = nxt("ps", NPS)
            for k in range(KC):
                p.op("pe", lambda e, ig=ig, c=c, k=k: e.matmul(ps[ig][:, :n], lhsT=W1[:, k, D + c * 128:D + (c + 1) * 128],
                                                               rhs=h[:, k, :n], start=(k == 0), stop=(k == KC - 1)),
                     r=[bW1[k], bh], w=[bps[ig]])
            j = nxt("sg", 2)
            p.op("act", lambda e, ig=ig, j=j, c=c: e.activation(out=sg[j][:, :n], in_=ps[ig][:, :n], func=AF.Sigmoid,
                                                                bias=V[:, 2, c:c + 1], scale=1.0),
                 r=[bps[ig], bV], w=[bsg[j]])
            p.op("dve", lambda e, ia=ia, j=j, c=c: e.scalar_tensor_tensor(out=U[ub][:, c, 0:n], in0=ps[ia][:, :n],
                                                                           scalar=V[:, 1, c:c + 1], in1=sg[j][:, :n],
                                                                           op0=ALU.add, op1=ALU.mult),
                 r=[bps[ia], bV, bsg[j]], w=[bU[ub]])

    def stageB(src, dst, t0, n, ub, mod, bmod, bgt, bbgt):
        for c in range(KC):
            i = nxt("ps", NPS)
            for tp in range(CW):
                p.op("pe", lambda e, i=i, c=c, tp=tp: e.matmul(ps[i][:, :n], lhsT=dg[:, c * CW + tp, :],
                                                               rhs=U[ub][:, c, tp:tp + n], start=(tp == 0), stop=(tp == CW - 1)),
                     r=[bdg, bU[ub]], w=[bps[i]])
            p.op("act", lambda e, i=i, c=c: e.activation(out=v[:, c, :n], in_=ps[i][:, :n], func=AF.Identity,
                                                         bias=V[:, 3, c:c + 1], scale=1.0),
                 r=[bps[i], bV], w=[bv[c]])
            j = nxt("v2", 2)
            p.op("dve", lambda e, j=j, c=c: e.tensor_tensor(out=v2[j][:, :n], in0=v[:, c, :n], in1=v[:, c, :n], op=ALU.mult),
                 r=[bv[c]], w=[bv2[j]])
            p.op("pe", lambda e, c=c: e.matmul(psm[:, :n], lhsT=ones_f[:], rhs=v[:, c, :n], start=(c == 0), stop=(c == KC - 1)),
                 r=[bones_f, bv[c]], w=[bpsm])
            p.op("pe", lambda e, c=c, j=j: e.matmul(psq[:, :n], lhsT=ones_f[:], rhs=v2[j][:, :n], start=(c == 0), stop=(c == KC - 1)),
                 r=[bones_f, bv2[j]], w=[bpsq])
        p.op("act", lambda e: e.activation(out=mean[:, :n], in_=psm[:, :n], func=AF.Copy, scale=1.0 / D), r=[bpsm], w=[bmean])
        p.op("dve", lambda e: e.tensor_tensor(out=lr[:, :n], in0=mean[:, :n], in1=mean[:, :n], op=ALU.mult), r=[bmean], w=[blr])
        p.op("dve", lambda e: e.scalar_tensor_tensor(out=lr[:, :n], in0=psq[:, :n], scalar=1.0 / D, in1=lr[:, :n],
                                                      op0=ALU.mult, op1=ALU.subtract), r=[bpsq, blr], w=[blr])
        p.op("act", lambda e: e.activation(out=lr[:, :n], in_=lr[:, :n], func=AF.Sqrt, bias=cm.eps[:, 0:1], scale=1.0),
             r=[blr, cm.b_eps], w=[blr])
        p.op("dve", lambda e: e.reciprocal(lr[:, :n], lr[:, :n]), r=[blr], w=[blr])
        for c in range(KC):
            j = nxt("t1", 2)
            p.op("dve", lambda e, j=j, c=c: e.tensor_tensor(out=t1[j][:, :n], in0=v[:, c, :n], in1=mean[:, :n], op=ALU.subtract),
                 r=[bv[c], bmean], w=[bt1[j]])
            p.op("dve", lambda e, j=j: e.tensor_tensor(out=t1[j][:, :n], in0=t1[j][:, :n], in1=lr[:, :n], op=ALU.mult),
                 r=[bt1[j], blr], w=[bt1[j]])
            p.op("act", lambda e, j=j, c=c: e.activation(out=s[:, c, :n], in_=t1[j][:, :n], func=AF.Silu,
                                                         bias=V[:, 5, c:c + 1], scale=V[:, 4, c:c + 1]),
                 r=[bt1[j], bV], w=[bs])
        for c in range(KC):
            i = nxt("ps", NPS)
            for k in range(KC):
                p.op("pe", lambda e, i=i, c=c, k=k: e.matmul(ps[i][:, :n], lhsT=W2[:, k, c * 128:(c + 1) * 128],
                                                             rhs=s[:, k, :n], start=(k == 0), stop=(k == KC - 1)),
                     r=[bW2, bs], w=[bps[i]])
            jx = nxt("xr", 2)
            p.dma(xr[jx][:, :n], src[c * 128:(c + 1) * 128, HALO + t0:HALO + t0 + n], w=[bxr[jx]], key=f"xr{jx}")
            p.op("dve", lambda e, jx=jx, c=c: e.tensor_scalar(out=xr[jx][:, :n], in0=xr[jx][:, :n], scalar1=bgt[:, c:c + 1],
                                                               scalar2=None, op0=ALU.add),
                 r=[bxr[jx], bbgt], w=[bxr[jx]])
            j = nxt("ob", 2)
            p.op("dve", lambda e, i=i, j=j, jx=jx, c=c: e.scalar_tensor_tensor(out=ob[j][:, :n], in0=ps[i][:, :n],
                                                                                 scalar=mod[:, 2, c:c + 1], in1=xr[jx][:, :n],
                                                                                 op0=ALU.mult, op1=ALU.add),
                 r=[bps[i], bmod, bxr[jx]], w=[bob[j]])
            p.dma(dst[c * 128:(c + 1) * 128, t0:t0 + n], ob[j][:, :n], r=[bob[j]], w=[bout], key=f"ob{j}")

    def run_seq(src, dst, Tn, A, bA, mod, bmod, msk, bmsk, bgt, bbgt):
        E = Tn + 2 * HALO
        nA = (E + NB - 1) // NB
        nB = (Tn + NB - 1) // NB

        def doA(b):
            e0 = b * NB
            n = min(NB, E - e0)
            ub = b % 3
            stageA(src, e0, n, ub, A, bA, mod, bmod)
            if b == 0:
                p.op("dve", lambda e, ub=ub: e.tensor_scalar(out=U[ub][:, :, 0:HALO], in0=U[ub][:, :, 0:HALO],
                                                              scalar1=msk[:, 0:1], scalar2=None, op0=ALU.mult),
                     r=[bU[ub], bmsk], w=[bU[ub]])
            lo = max(E - HALO, e0) - e0
            hi = min(E, e0 + n) - e0
            if hi > lo:
                p.op("dve", lambda e, ub=ub, lo=lo, hi=hi: e.tensor_scalar(out=U[ub][:, :, lo:hi], in0=U[ub][:, :, lo:hi],
                                                                            scalar1=msk[:, 1:2], scalar2=None, op0=ALU.mult),
                     r=[bU[ub], bmsk], w=[bU[ub]])
            if b >= 1:
                m = min(2 * HALO, n)
                pb = (b - 1) % 3
                p.op("dve", lambda e, ub=ub, pb=pb, m=m: e.tensor_copy(U[pb][:, :, NB:NB + m], U[ub][:, :, 0:m]),
                     r=[bU[ub]], w=[bU[pb]])

        doA(0)
        for b in range(nB):
            if b + 1 < nA:
                doA(b + 1)
            t0 = b * NB
            n = min(NB, Tn - t0)
            stageB(src, dst, t0, n, b % 3, mod, bmod, bgt, bbgt)

    run_seq(ce, ocT, TC, Ac, bAc, mc_, bmc, zero2, bz2, bgc, bbgc)
    run_seq(xe, oT, T, Al, bAl, ml, bml, hm, bhm, bgl, bbgl)
    p.wait_all("sp", bob)
    p.emit()
    p.close()
    return nc


HD = 64
ATTN_SCALE = HD ** -0.5


def build_at1(T=4096, TC=256, NB=512):
    nc = bass.Bass("TRN2", target_bir_lowering=False)
    xT = nc.dram_tensor("xT", [D, T], F32, kind="ExternalInput").ap()
    cT = nc.dram_tensor("cT", [D, TC], F32, kind="ExternalInput").ap()
    wqkv = nc.dram_tensor("wqkv", [D, 1536], BF16, kind="ExternalInput").ap()
    gn = nc.dram_tensor("gn", [128, 1, KC], F32, kind="ExternalInput").ap()
    modl = nc.dram_tensor("modl", [128, 3, KC], F32, kind="ExternalInput").ap()
    modc = nc.dram_tensor("modc", [128, 3, KC], F32, kind="ExternalInput").ap()
    gqk = nc.dram_tensor("gqk", [128, 2], F32, kind="ExternalInput").ap()
    gqkb = nc.dram_tensor("gqkb", [128, 128], F32, kind="ExternalInput").ap()
    cosT = nc.dram_tensor("cosT", [128, T], F32, kind="ExternalInput").ap()
    sinT = nc.dram_tensor("sinT", [128, T], F32, kind="ExternalInput").ap()
    bdo = nc.dram_tensor("bdo", [128, 128], F32, kind="ExternalInput").ap()
    qT = nc.dram_tensor("qT", [D, T], BF16, kind="ExternalOutput").ap()
    kT = nc.dram_tensor("kT", [256, T], BF16, kind="ExternalOutput").ap()
    vv = nc.dram_tensor("v", [T, 256], BF16, kind="ExternalOutput").ap()
    qcT = nc.dram_tensor("qcT", [D, TC], BF16, kind="ExternalOutput").ap()
    kcT = nc.dram_tensor("kcT", [256, TC], BF16, kind="ExternalOutput").ap()
    vc = nc.dram_tensor("vc", [TC, 256], BF16, kind="ExternalOutput").ap()
    negm = nc.dram_tensor("negm", [128, 1], F32, kind="ExternalOutput").ap()
    p = Prog(nc)
    cm = Common(p, NB)
    W = p.sbuf("W", [128, KC, 1536], BF16)
    bW = p.buf("W")
    p.dma(W[:], wqkv.rearrange("(k p) d -> p k d", p=128), w=[bW], key="W")
    g, bg = cm.load_vec("gn", gn, [128, 1, KC])
    ml, bml = cm.load_vec("modl", modl, [128, 3, KC])
    mc_, bmc = cm.load_vec("modc", modc, [128, 3, KC])
    gq, bgq = cm.load_vec("gqk", gqk, [128, 2])
    gb, bgb = cm.load_vec("gqkb", gqkb, [128, 128])
    bdf, bbdf = cm.load_vec("bdo", bdo, [128, 128])
    bd = p.sbuf("bd_bf", [128, 128], BF16)
    bbd = p.buf("bd_bf")
    p.op("dve", lambda e: e.tensor_copy(bd[:], bdf[:]), r=[bbdf], w=[bbd])
    Al, bAl = cm.make_AB("l", g, bg, 0, ml, bml, 1, 0)
    Ac, bAc = cm.make_AB("c", g, bg, 0, mc_, bmc, 1, 0)
    mx = p.sbuf("mx", [128, 2], F32)
    bmx = p.buf("mx")
    p.op("dve", lambda e: e.reduce_max(out=mx[:, 0:1], in_=gb[:, 0:64], axis=AX.X, apply_absolute_value=True), r=[bgb], w=[bmx])
    p.op("dve", lambda e: e.reduce_max(out=mx[:, 1:2], in_=gb[:, 64:128], axis=AX.X, apply_absolute_value=True), r=[bmx, bgb], w=[bmx])
    nm = p.sbuf("nm", [128, 1], F32)
    bnm = p.buf("nm")
    p.op("dve", lambda e: e.scalar_tensor_tensor(out=nm[:], in0=mx[:, 0:1], scalar=-8.0, in1=mx[:, 1:2], op0=ALU.mult, op1=ALU.mult),
         r=[bmx], w=[bnm])
    bnegm = p.buf("negm_out")
    p.dma(negm, nm[:], r=[bnm], w=[bnegm], key="negm")

    xin = [p.sbuf(f"xin{k}", [128, NB], F32) for k in range(KC)]
    bxin = [p.buf(f"xin{k}") for k in range(KC)]
    h = p.sbuf("h", [128, KC, NB], BF16)
    bh = p.buf("h")
    cs = p.sbuf("cs", [128, NB], F32)
    bcs = p.buf("cs")
    sn = p.sbuf("sn", [128, NB], F32)
    bsn = p.buf("sn")
    sqb = [p.sbuf(f"sqb{i}", [128, NB], BF16) for i in range(2)]
    bsqb = [p.buf(f"sqb{i}") for i in range(2)]
    rs = [p.sbuf(f"rs{i}", [128, NB], F32) for i in range(2)]
    brs = [p.buf(f"rs{i}") for i in range(2)]
    qn = [p.sbuf(f"qn{i}", [128, NB], F32) for i in range(2)]
    bqn = [p.buf(f"qn{i}") for i in range(2)]
    ta = [p.sbuf(f"ta{i}", [128, NB], F32) for i in range(2)]
    bta = [p.buf(f"ta{i}") for i in range(2)]
    tb = [p.sbuf(f"tb{i}", [128, NB], F32) for i in range(2)]
    btb = [p.buf(f"tb{i}") for i in range(2)]
    qo = [p.sbuf(f"qo{i}", [128, NB], BF16) for i in range(2)]
    bqo = [p.buf(f"qo{i}") for i in range(2)]
    vo = [p.sbuf(f"vo{i}", [128, 256], BF16) for i in range(2)]
    bvo = [p.buf(f"vo{i}") for i in range(2)]
    NPS = 2
    ps = [p.psum(f"ps{i}", [128, NB]) for i in range(NPS)]
    bps = [p.buf(f"ps{i}") for i in range(NPS)]
    pss = [p.psum(f"pss{i}", [128, NB]) for i in range(2)]
    bpss = [p.buf(f"pss{i}") for i in range(2)]
    psv = [p.psum(f"psv{i}", [128, 256]) for i in range(2)]
    bpsv = [p.buf(f"psv{i}") for i in range(2)]
    bout = p.buf("out")
    cnt = {}

    def nxt(name, n):
        i = cnt.get(name, 0) % n
        cnt[name] = cnt.get(name, 0) + 1
        return i

    def block(src, t0, n, A, bA, mod, bmod, rope, dq, dk, dv):
        xs = [xin[k][:, :n] for k in range(KC)]
        for k in range(KC):
            p.dma(xin[k][:, :n], src[k * 128:(k + 1) * 128, t0:t0 + n], w=[bxin[k]], key=f"xin{k}")
        if rope:
            p.dma(cs[:, :n], cosT[:, t0:t0 + n], w=[bcs], key="cs")
            p.dma(sn[:, :n], sinT[:, t0:t0 + n], w=[bsn], key="sn")
        cm.norm_mod(xs, bxin, n, A, bA, mod, bmod, 0, h, bh)
        for c in range(10):
            i = nxt("ps", NPS)
            for k in range(KC):
                p.op("pe", lambda e, i=i, c=c, k=k: e.matmul(ps[i][:, :n], lhsT=W[:, k, c * 128:(c + 1) * 128],
                                                             rhs=h[:, k, :n], start=(k == 0), stop=(k == KC - 1)),
                     r=[bW, bh], w=[bps[i]])
            j = nxt("sqb", 2)
            p.op("act", lambda e, i=i, j=j: e.activation(out=sqb[j][:, :n], in_=ps[i][:, :n], func=AF.Square),
                 r=[bps[i]], w=[bsqb[j]])
            js = nxt("pss", 2)
            p.op("pe", lambda e, j=j, js=js: e.matmul(pss[js][:, :n], lhsT=bd[:], rhs=sqb[j][:, :n], start=True, stop=True),
                 r=[bbd, bsqb[j]], w=[bpss[js]])
            jr = nxt("rs", 2)
            p.op("act", lambda e, js=js, jr=jr: e.activation(out=rs[jr][:, :n], in_=pss[js][:, :n], func=AF.Sqrt,
                                                             bias=cm.eps[:, 0:1], scale=1.0 / HD),
                 r=[bpss[js], cm.b_eps], w=[brs[jr]])
            p.op("dve", lambda e, jr=jr: e.reciprocal(rs[jr][:, :n], rs[jr][:, :n]), r=[brs[jr]], w=[brs[jr]])
            gi = 0 if c < 8 else 1
            jq = nxt("qn", 2)
            jo = nxt("qo", 2)
            dst = dq[c * 128:(c + 1) * 128, t0:t0 + n] if c < 8 else dk[(c - 8) * 128:(c - 7) * 128, t0:t0 + n]
            if rope:
                p.op("dve", lambda e, i=i, jr=jr, jq=jq, gi=gi: e.scalar_tensor_tensor(out=qn[jq][:, :n], in0=ps[i][:, :n],
                                                                                       scalar=gq[:, gi:gi + 1], in1=rs[jr][:, :n],
                                                                                       op0=ALU.mult, op1=ALU.mult),
                     r=[bps[i], bgq, brs[jr]], w=[bqn[jq]])
                ja = nxt("ta", 2)
                p.op("dve", lambda e, jq=jq, ja=ja: e.tensor_tensor(out=ta[ja][:, :n], in0=qn[jq][:, :n], in1=cs[:, :n], op=ALU.mult),
                     r=[bqn[jq], bcs], w=[bta[ja]])
                jb = nxt("tb", 2)
                for (o0, s0) in ((0, 32), (32, 0), (64, 96), (96, 64)):
                    p.op("dve",
                         lambda e, jq=jq, jb=jb, o0=o0, s0=s0: e.tensor_tensor(out=tb[jb][o0:o0 + 32, :n], in0=qn[jq][s0:s0 + 32, :n],
                                                                               in1=sn[s0:s0 + 32, :n], op=ALU.mult),
                         r=[bqn[jq], bsn], w=[btb[jb]])
                p.op("dve", lambda e, ja=ja, jb=jb, jo=jo: e.tensor_tensor(out=qo[jo][:, :n], in0=ta[ja][:, :n], in1=tb[jb][:, :n], op=ALU.add),
                     r=[bta[ja], btb[jb]], w=[bqo[jo]])
            else:
                p.op("dve", lambda e, i=i, jr=jr, jo=jo, gi=gi: e.scalar_tensor_tensor(out=qo[jo][:, :n], in0=ps[i][:, :n],
                                                                                       scalar=gq[:, gi:gi + 1], in1=rs[jr][:, :n],
                                                                                       op0=ALU.mult, op1=ALU.mult),
                     r=[bps[i], bgq, brs[jr]], w=[bqo[jo]])
            p.dma(dst, qo[jo][:, :n], r=[bqo[jo]], w=[bout], key=f"qo{jo}")
        for j4 in range(n // 128):
            iv = nxt("psv", 2)
            for k in range(KC):
                p.op("pe", lambda e, iv=iv, j4=j4, k=k: e.matmul(psv[iv][:, :], lhsT=h[:, k, j4 * 128:(j4 + 1) * 128],
                                                                 rhs=W[:, k, 1280:1536], start=(k == 0), stop=(k == KC - 1)),
                     r=[bW, bh], w=[bpsv[iv]])
            jv = nxt("vo", 2)
            p.op("act", lambda e, iv=iv, jv=jv: e.activation(out=vo[jv][:], in_=psv[iv][:], func=AF.Copy),
                 r=[bpsv[iv]], w=[bvo[jv]])
            p.dma(dv[t0 + j4 * 128:t0 + (j4 + 1) * 128, :], vo[jv][:], r=[bvo[jv]], w=[bout], key=f"vo{jv}")

    block(cT, 0, TC, Ac, bAc, mc_, bmc, False, qcT, kcT, vc)
    for b in range(T // NB):
        block(xT, b * NB, NB, Al, bAl, ml, bml, True, qT, kT, vv)
    p.wait_all("sp", bqo + bvo + [bnm])
    p.emit()
    p.close()
    return nc


def build_at2(TQ=16384, TC=256, NQ=512):
    TK = TC + TQ
    NKB = TK // 128
    NCB = TC // 128
    G = 3
    nc = bass.Bass("TRN2", target_bir_lowering=False)
    q4 = nc.dram_tensor("q4", [256, TQ], BF16, kind="ExternalInput").ap()
    qc4 = nc.dram_tensor("qc4", [256, TC], BF16, kind="ExternalInput").ap()
    kT = nc.dram_tensor("kT", [64, TK], BF16, kind="ExternalInput").ap()
    vx = nc.dram_tensor("vx", [TK, 128], BF16, kind="ExternalInput").ap()
    negm = nc.dram_tensor("negm", [128, 1], F32, kind="ExternalInput").ap()
    o4 = nc.dram_tensor("o4", [256, TQ], BF16, kind="ExternalOutput").ap()
    oc4 = nc.dram_tensor("oc4", [256, TC], BF16, kind="ExternalOutput").ap()
    p = Prog(nc)
    K = p.sbuf("K", [64, TK], BF16)
    bK = p.buf("K")
    V = p.sbuf("V", [128, NKB, 128], BF16)
    bV = p.buf("V")
    NLD = 8
    for i in range(NLD):
        a, b = i * TK // NLD, (i + 1) * TK // NLD
        p.dma(K[:, a:b], kT[:, a:b], w=[p.buf()], key=f"K{i}")
    bK_all = []
    bKs = [p.buf(f"K{i}") for i in range(NLD)]
    p.q["sp"] = []
    p.dma_cnt.clear()
    for i in range(NLD):
        a, b = i * TK // NLD, (i + 1) * TK // NLD
        p.dma(K[:, a:b], kT[:, a:b], w=[bKs[i]], key=f"K{i}")
    vxv = vx.rearrange("(n p) d -> p n d", p=128)
    bVs = [p.buf(f"V{i}") for i in range(NLD)]
    for i in range(NLD):
        a, b = i * NKB // NLD, (i + 1) * NKB // NLD
        p.dma(V[:, a:b, :], vxv[:, a:b, :], w=[bVs[i]], key=f"V{i}")
    kb_buf = lambda kb: bKs[min(NLD - 1, (kb * 128 * NLD) // TK)]
    kb_buf2 = lambda kb: bKs[min(NLD - 1, ((kb * 128 + 127) * NLD) // TK)]
    vb_buf = lambda kb: [bVs[i] for i in range(NLD) if i * NKB // NLD <= kb < (i + 1) * NKB // NLD][0]
    nm = p.sbuf("nm", [128, 1], F32)
    bnm = p.buf("nm")
    p.dma(nm[:], negm, w=[bnm], key="nm")
    qs = [p.sbuf(f"qs{i}", [64, NQ], BF16) for i in range(3)]
    bqs = [p.buf(f"qs{i}") for i in range(3)]
    P = [p.sbuf(f"P{i}", [128, G, NQ], BF16) for i in range(2)]
    bP = [p.buf(f"P{i}") for i in range(2)]
    pss = [p.psum(f"pss{i}", [128, G, NQ]) for i in range(2)]
    bpss = [p.buf(f"pss{i}") for i in range(2)]
    pso = [p.psum(f"pso{i}", [128, NQ]) for i in range(2)]
    bpso = [p.buf(f"pso{i}") for i in range(2)]
    rec = [p.sbuf(f"rec{i}", [128, NQ], F32) for i in range(2)]
    brec = [p.buf(f"rec{i}") for i in range(2)]
    ob = [p.sbuf(f"ob{i}", [64, NQ], BF16) for i in range(2)]
    bob = [p.buf(f"ob{i}") for i in range(2)]
    bout = p.buf("out")
    cnt = {}

    def nxt(name, n):
        i = cnt.get(name, 0) % n
        cnt[name] = cnt.get(name, 0) + 1
        return i

    def qblock(qsrc, odst, hd, t0, n, kbs):
        iq = nxt("qs", 3)
        p.dma(qs[iq][:, :n], qsrc[hd * 64:(hd + 1) * 64, t0:t0 + n], w=[bqs[iq]], key=f"qs{iq}")
        io = nxt("pso", 2)
        groups = [kbs[i:i + G] for i in range(0, len(kbs), G)]
        first = True
        for gi, grp in enumerate(groups):
            isx = nxt("pss", 2)
            for j, kb in enumerate(grp):
                p.op("pe", lambda e, isx=isx, j=j, kb=kb, iq=iq: e.matmul(pss[isx][:, j, :n], lhsT=K[:, kb * 128:(kb + 1) * 128],
                                                                          rhs=qs[iq][:, :n], start=True, stop=True),
                     r=[kb_buf(kb), kb_buf2(kb), bqs[iq]], w=[bpss[isx]])
            ip = nxt("P", 2)
            ng = len(grp)
            p.op("act", lambda e, isx=isx, ip=ip, ng=ng: e.activation(out=P[ip][:, 0:ng, :n], in_=pss[isx][:, 0:ng, :n], func=AF.Exp,
                                                                      bias=nm[:, 0:1], scale=ATTN_SCALE),
                 r=[bpss[isx], bnm], w=[bP[ip]])
            for j, kb in enumerate(grp):
                last = (gi == len(groups) - 1) and (j == ng - 1)
                p.op("pe", lambda e, io=io, ip=ip, j=j, kb=kb, first=first, last=last: e.matmul(
                    pso[io][:, :n], lhsT=V[:, kb, :], rhs=P[ip][:, j, :n], start=first, stop=last),
                     r=[vb_buf(kb), bP[ip]], w=[bpso[io]])
                first = False
        ir = nxt("rec", 2)
        p.op("dve", lambda e, io=io, ir=ir: e.reciprocal(rec[ir][64:128, :n], pso[io][64:128, :n]), r=[bpso[io]], w=[brec[ir]])
        jo = nxt("ob", 2)
        p.op("dve", lambda e, io=io, ir=ir, jo=jo: e.tensor_tensor(out=ob[jo][:, :n], in0=pso[io][0:64, :n], in1=rec[ir][64:128, :n], op=ALU.mult),
             r=[bpso[io], brec[ir]], w=[bob[jo]])
        p.dma(odst[hd * 64:(hd + 1) * 64, t0:t0 + n], ob[jo][:, :n], r=[bob[jo]], w=[bout], key=f"ob{jo}")

    for hd in range(4):
        qblock(qc4, oc4, hd, 0, TC, list(range(NCB)))
    for qb in range(TQ // NQ):
        for hd in range(4):
            qblock(q4, o4, hd, qb * NQ, NQ, list(range(NKB)))
    p.wait_all("sp", bob)
    p.emit()
    p.close()
    return nc


DK = 128
DV = 256
NH = 4
TAU = 16.0


def build_gp1(T=4096, TC=256, NB=512):
    nc = bass.Bass("TRN2", target_bir_lowering=False)
    xT = nc.dram_tensor("xT", [D, T], F32, kind="ExternalInput").ap()
    cT = nc.dram_tensor("cT", [D, TC], F32, kind="ExternalInput").ap()
    win = nc.dram_tensor("win", [D, 3072], BF16, kind="ExternalInput").ap()
    wg1 = nc.dram_tensor("wg1", [D, 32], BF16, kind="ExternalInput").ap()
    wg2 = nc.dram_tensor("wg2", [16, 2, 512], BF16, kind="ExternalInput").ap()
    bgb = nc.dram_tensor("bgb", [128, 2, 512], F32, kind="ExternalInput").ap()
    gn = nc.dram_tensor("gn", [128, 1, KC], F32, kind="ExternalInput").ap()
    modl = nc.dram_tensor("modl", [128, 3, KC], F32, kind="ExternalInput").ap()
    modc = nc.dram_tensor("modc", [128, 3, KC], F32, kind="ExternalInput").ap()
    tri = nc.dram_tensor("tri", [128, 4, 128], F32, kind="ExternalInput").ap()
    outs = {}
    for sfx, Tn in (("", T), ("c", TC)):
        outs["qb" + sfx] = nc.dram_tensor("qb" + sfx, [8, 128, Tn], BF16, kind="ExternalOutput").ap()
        outs["kb" + sfx] = nc.dram_tensor("kb" + sfx, [8, 128, Tn], BF16, kind="ExternalOutput").ap()
        outs["kd" + sfx] = nc.dram_tensor("kd" + sfx, [2, Tn, 512], BF16, kind="ExternalOutput").ap()
        outs["vt" + sfx] = nc.dram_tensor("vt" + sfx, [Tn, 1024], BF16, kind="ExternalOutput").ap()
        outs["dec" + sfx] = nc.dram_tensor("dec" + sfx, [128, 8, Tn // 128], F32, kind="ExternalOutput").ap()
        outs["sr" + sfx] = nc.dram_tensor("sr" + sfx, [D, Tn], BF16, kind="ExternalOutput").ap()
    p = Prog(nc)
    cm = Common(p, NB)
    W = p.sbuf("W", [128, KC, 3072], BF16)
    bW = [p.buf(f"W{k}") for k in range(KC)]
    for k in range(KC):
        p.dma(W[:, k, :], win[k * 128:(k + 1) * 128, :], w=[bW[k]], key=f"W{k}")
    bWall = bW
    Wg1 = p.sbuf("Wg1", [128, KC, 32], BF16)
    bWg1 = p.buf("Wg1")
    p.dma(Wg1[:], wg1.rearrange("(k p) d -> p k d", p=128), w=[bWg1], key="Wg1")
    Wg2 = p.sbuf("Wg2", [16, 2, 512], BF16)
    bWg2 = p.buf("Wg2")
    p.dma(Wg2[:], wg2, w=[bWg2], key="Wg2")
    bgt, bbgt = cm.load_vec("bgb", bgb, [128, 2, 512])
    g, bg = cm.load_vec("gn", gn, [128, 1, KC])
    ml, bml = cm.load_vec("modl", modl, [128, 3, KC])
    mc_, bmc = cm.load_vec("modc", modc, [128, 3, KC])
    trif, btrif = cm.load_vec("tri", tri, [128, 4, 128])
    trib = p.sbuf("trib", [128, 4, 128], BF16)
    btri = p.buf("trib")
    p.op("dve", lambda e: e.tensor_copy(trib[:], trif[:]), r=[btrif], w=[btri])
    one1 = p.sbuf("one1", [128, 1], F32)
    bone1 = p.buf("one1")
    p.op("dve", lambda e: e.memset(one1[:], 1.0), w=[bone1])
    Al, bAl = cm.make_AB("l", g, bg, 0, ml, bml, 1, 0)
    Ac, bAc = cm.make_AB("c", g, bg, 0, mc_, bmc, 1, 0)

    xin = [p.sbuf(f"xin{k}", [128, NB], F32) for k in range(KC)]
    bxin = [p.buf(f"xin{k}") for k in range(KC)]
    h = p.sbuf("h", [128, KC, NB], BF16)
    bh = p.buf("h")
    qf = [p.sbuf(f"qf{i}", [128, NB], F32) for i in range(NH)]
    bqf = [p.buf(f"qf{i}") for i in range(NH)]
    kf = [p.sbuf(f"kf{i}", [128, NB], F32) for i in range(NH)]
    bkf = [p.buf(f"kf{i}") for i in range(NH)]
    ktm = [p.sbuf(f"ktm{i}", [128, 512], F32) for i in range(4)]
    bktm = [p.buf(f"ktm{i}") for i in range(4)]
    lg = [p.sbuf(f"lg{i}", [128, 512], BF16) for i in range(4)]
    blg = [p.buf(f"lg{i}") for i in range(4)]
    hgs = p.sbuf("hgs", [16, NB], BF16)
    bhgs = p.buf("hgs")
    f32t = [p.sbuf(f"f32t{i}", [128, NB], F32) for i in range(8)]
    bf32t = [p.buf(f"f32t{i}") for i in range(8)]
    bft = [p.sbuf(f"bft{i}", [128, NB], BF16) for i in range(10)]
    bbft = [p.buf(f"bft{i}") for i in range(10)]
    dect = {sfx: p.sbuf("dect" + sfx, [128, 8, Tn // 128], F32) for sfx, Tn in (("", T), ("c", TC))}
    bdect = {sfx: p.buf("dect" + sfx) for sfx in ("", "c")}
    NPS = 4
    ps = [p.psum(f"ps{i}", [128, NB]) for i in range(NPS)]
    bps = [p.buf(f"ps{i}") for i in range(NPS)]
    psh = p.psum("psh", [16, NB])
    bpsh = p.buf("psh")
    bout = p.buf("out")
    cnt = {}

    def nxt(name, n):
        i = cnt.get(name, 0) % n
        cnt[name] = cnt.get(name, 0) + 1
        return i

    def mm_fm(col0, n):
        i = nxt("ps", NPS)
        for k in range(KC):
            p.op("pe", lambda e, i=i, k=k: e.matmul(ps[i][:, :n], lhsT=W[:, k, col0:col0 + 128], rhs=h[:, k, :n],
                                                    start=(k == 0), stop=(k == KC - 1)), r=[bW[k], bh], w=[bps[i]])
        return i

    def mm_tm(col0, j):
        i = nxt("ps", NPS)
        for k in range(KC):
            p.op("pe", lambda e, i=i, k=k: e.matmul(ps[i][:, :512], lhsT=h[:, k, j * 128:(j + 1) * 128], rhs=W[:, k, col0:col0 + 512],
                                                    start=(k == 0), stop=(k == KC - 1)), r=[bW[k], bh], w=[bps[i]])
        return i

    def block(src, sfx, t0, n, A, bA, mod, bmod):
        nch = n // 128
        xs = [xin[k][:, :n] for k in range(KC)]
        for k in range(KC):
            p.dma(xin[k][:, :n], src[k * 128:(k + 1) * 128, t0:t0 + n], w=[bxin[k]], key=f"xin{k}")
        cm.norm_mod(xs, bxin, n, A, bA, mod, bmod, 0, h, bh)
        for hh in range(NH):
            i = mm_fm(hh * 128, n)
            p.op("act", lambda e, i=i, hh=hh: e.activation(out=qf[hh][:, :n], in_=ps[i][:, :n], func=AF.Copy, scale=DK ** -0.5),
                 r=[bps[i]], w=[bqf[hh]])
            i = mm_fm(512 + hh * 128, n)
            p.op("dve", lambda e, i=i, hh=hh: e.tensor_copy(kf[hh][:, :n], ps[i][:, :n]), r=[bps[i]], w=[bkf[hh]])
        for c in range(KC):
            i = mm_fm(2048 + c * 128, n)
            jb = nxt("bft", 10)
            p.op("act", lambda e, i=i, jb=jb: e.activation(out=bft[jb][:, :n], in_=ps[i][:, :n], func=AF.Silu), r=[bps[i]], w=[bbft[jb]])
            p.dma(outs["sr" + sfx][c * 128:(c + 1) * 128, t0:t0 + n], bft[jb][:, :n], r=[bbft[jb]], w=[bout], key=f"bft{jb}")
        for j in range(nch):
            i = mm_tm(512, j)
            p.op("act", lambda e, i=i, j=j: e.activation(out=ktm[j][:], in_=ps[i][:, :512], func=AF.Copy), r=[bps[i]], w=[bktm[j]])
            for hf in range(2):
                i = mm_tm(1024 + hf * 512, j)
                jb = nxt("bft", 10)
                p.op("dve", lambda e, i=i, jb=jb: e.tensor_copy(bft[jb][:, :512], ps[i][:, :512]), r=[bps[i]], w=[bbft[jb]])
                p.dma(outs["vt" + sfx][t0 + j * 128:t0 + (j + 1) * 128, hf * 512:(hf + 1) * 512], bft[jb][:, :512],
                      r=[bbft[jb]], w=[bout], key=f"bft{jb}")
        for d in range(2):
            for k in range(KC):
                p.op("pe", lambda e, k=k, d=d: e.matmul(psh[:, :n], lhsT=Wg1[:, k, d * 16:(d + 1) * 16], rhs=h[:, k, :n],
                                                        start=(k == 0), stop=(k == KC - 1)), r=[bWg1, bh], w=[bpsh])
            p.op("act", lambda e: e.activation(out=hgs[:, :n], in_=psh[:, :n], func=AF.Copy), r=[bpsh], w=[bhgs])
            for j in range(nch):
                i = nxt("ps", NPS)
                p.op("pe", lambda e, i=i, j=j, d=d: e.matmul(ps[i][:, :512], lhsT=hgs[:, j * 128:(j + 1) * 128], rhs=Wg2[:, d, :],
                                                             start=True, stop=True), r=[bhgs, bWg2], w=[bps[i]])
                jf = nxt("f32t", 8)
                p.op("dve", lambda e, i=i, jf=jf, d=d: e.tensor_tensor(out=f32t[jf][:, :512], in0=ps[i][:, :512], in1=bgt[:, d, :], op=ALU.add),
                     r=[bps[i], bbgt], w=[bf32t[jf]])
                p.op("act", lambda e, jf=jf: e.activation(out=f32t[jf][:, :512], in_=f32t[jf][:, :512], func=AF.Exp, scale=-1.0),
                     r=[bf32t[jf]], w=[bf32t[jf]])
                p.op("act", lambda e, jf=jf: e.activation(out=f32t[jf][:, :512], in_=f32t[jf][:, :512], func=AF.Ln, bias=one1[:, 0:1], scale=1.0),
                     r=[bf32t[jf], bone1], w=[bf32t[jf]])
                p.op("dve", lambda e, jf=jf, j=j: e.tensor_scalar(out=lg[j][:], in0=f32t[jf][:, :512], scalar1=-1.0 / TAU, scalar2=None, op0=ALU.mult),
                     r=[bf32t[jf]], w=[blg[j]])
                i2 = nxt("ps", NPS)
                p.op("pe", lambda e, i2=i2, j=j, d=d: e.matmul(ps[i2][:, :512], lhsT=trib[:, 2 + d, :], rhs=lg[j][:], start=True, stop=True),
                     r=[btri, blg[j]], w=[bps[i2]])
                jf2 = nxt("f32t", 8)
                p.op("act", lambda e, i2=i2, jf2=jf2: e.activation(out=f32t[jf2][:, :512], in_=ps[i2][:, :512], func=AF.Exp),
                     r=[bps[i2]], w=[bf32t[jf2]])
                jb = nxt("bft", 10)
                p.op("dve", lambda e, jf2=jf2, jb=jb, j=j: e.tensor_tensor(out=bft[jb][:, :512], in0=ktm[j][:], in1=f32t[jf2][:, :512], op=ALU.mult),
                     r=[bf32t[jf2], bktm[j]], w=[bbft[jb]])
                p.dma(outs["kd" + sfx][d, t0 + j * 128:t0 + (j + 1) * 128, :], bft[jb][:, :512], r=[bbft[jb]], w=[bout], key=f"bft{jb}")
            for hh in range(NH):
                i = nxt("ps", NPS)
                for j in range(nch):
                    p.op("pe", lambda e, i=i, j=j, hh=hh, d=d: e.matmul(ps[i][:, j * 128:(j + 1) * 128], lhsT=lg[j][:, hh * 128:(hh + 1) * 128],
                                                                        rhs=trib[:, d, :], start=True, stop=True),
                         r=[blg[j], btri], w=[bps[i]])
                jf = nxt("f32t", 8)
                p.op("act", lambda e, i=i, jf=jf: e.activation(out=f32t[jf][:, :n], in_=ps[i][:, :n], func=AF.Exp), r=[bps[i]], w=[bf32t[jf]])
                jf2 = nxt("f32t", 8)
                p.op("act", lambda e, i=i, jf2=jf2: e.activation(out=f32t[jf2][:, :n], in_=ps[i][:, :n], func=AF.Exp, scale=-1.0),
                     r=[bps[i]], w=[bf32t[jf2]])
                jb = nxt("bft", 10)
                p.op("dve", lambda e, jf=jf, jb=jb, hh=hh: e.tensor_tensor(out=bft[jb][:, :n], in0=qf[hh][:, :n], in1=f32t[jf][:, :n], op=ALU.mult),
                     r=[bf32t[jf], bqf[hh]], w=[bbft[jb]])
                p.dma(outs["qb" + sfx][d * 4 + hh, :, t0:t0 + n], bft[jb][:, :n], r=[bbft[jb]], w=[bout], key=f"bft{jb}")
                jb2 = nxt("bft", 10)
                p.op("dve", lambda e, jf2=jf2, jb2=jb2, hh=hh: e.tensor_tensor(out=bft[jb2][:, :n], in0=kf[hh][:, :n], in1=f32t[jf2][:, :n], op=ALU.mult),
                     r=[bf32t[jf2], bkf[hh]], w=[bbft[jb2]])
                p.dma(outs["kb" + sfx][d * 4 + hh, :, t0:t0 + n], bft[jb2][:, :n], r=[bbft[jb2]], w=[bout], key=f"bft{jb2}")
                col = 127 if d == 0 else 0
                c0 = t0 // 128
                p.op("dve", lambda e, jf=jf, hh=hh, d=d, col=col, c0=c0: e.tensor_copy(
                    dect[sfx][:, d * 4 + hh, c0:c0 + nch],
                    f32t[jf][:, :n].rearrange("p (c t) -> p c t", t=128)[:, :, col]),
                     r=[bf32t[jf]], w=[bdect[sfx]])

    block(cT, "c", 0, TC, Ac, bAc, mc_, bmc)
    for b in range(T // NB):
        block(xT, "", b * NB, NB, Al, bAl, ml, bml)
    bdo = p.buf("decout")
    for sfx in ("", "c"):
        p.dma(outs["dec" + sfx], dect[sfx][:], r=[bdect[sfx]], w=[bdo], key="dec" + sfx)
    p.wait_all("sp", bbft + [bdo])
    p.emit()
    p.close()
    return nc


def build_gs2(TQ=16384, TC=256):
    TK = TQ + TC
    NCH = TK // 128
    NCC = TC // 128
    GC = 4
    nc = bass.Bass("TRN2", target_bir_lowering=False)
    qb = nc.dram_tensor("qb", [2, 128, TK], BF16, kind="ExternalInput").ap()
    kb = nc.dram_tensor("kb", [2, 128, TK], BF16, kind="ExternalInput").ap()
    kd = nc.dram_tensor("kd", [2, TK, 128], BF16, kind="ExternalInput").ap()
    vt = nc.dram_tensor("vt", [TK, 256], BF16, kind="ExternalInput").ap()
    dec = nc.dram_tensor("dec", [128, 2, NCH], F32, kind="ExternalInput").ap()
    msk = nc.dram_tensor("msk", [128, 2, 128], F32, kind="ExternalInput").ap()
    oT = nc.dram_tensor("oT", [2, 256, TK], F32, kind="ExternalOutput").ap()
    p = Prog(nc)
    dc = p.sbuf("dc", [128, 2, NCH], F32)
    bdc = p.buf("dc")
    p.dma(dc[:], dec, w=[bdc], key="dc")
    mk = p.sbuf("mk", [128, 2, 128], F32)
    bmk = p.buf("mk")
    p.dma(mk[:], msk, w=[bmk], key="mk")
    S = [p.sbuf(f"S{d}", [128, 256], F32) for d in range(2)]
    bS = [p.buf(f"S{d}") for d in range(2)]
    Sb = [p.sbuf(f"Sb{d}", [128, 256], BF16) for d in range(2)]
    bSb = [p.buf(f"Sb{d}") for d in range(2)]
    for d in range(2):
        p.op("dve", lambda e, d=d: e.memset(S[d][:], 0.0), w=[bS[d]])
        p.op("dve", lambda e, d=d: e.memset(Sb[d][:], 0.0), w=[bSb[d]])
    NBUF = 2
    qg = [[p.sbuf(f"qg{d}_{i}", [128, GC * 128], BF16) for i in range(NBUF)] for d in range(2)]
    kg = [[p.sbuf(f"kg{d}_{i}", [128, GC * 128], BF16) for i in range(NBUF)] for d in range(2)]
    kdg = [[p.sbuf(f"kdg{d}_{i}", [128, GC, 128], BF16) for i in range(NBUF)] for d in range(2)]
    vg = [[p.sbuf(f"vg{d}_{i}", [128, GC, 256], BF16) for i in range(NBUF)] for d in range(2)]
    og = [[p.sbuf(f"og{d}_{i}", [128, 2, GC * 128], F32) for i in range(NBUF)] for d in range(2)]
    bin_ = [[p.buf(f"in{d}_{i}") for i in range(NBUF)] for d in range(2)]
    bog = [[p.buf(f"og{d}_{i}") for i in range(NBUF)] for d in range(2)]
    Am = [[p.sbuf(f"Am{d}_{i}", [128, 128], BF16) for i in range(2)] for d in range(2)]
    bAm = [[p.buf(f"Am{d}_{i}") for i in range(2)] for d in range(2)]
    psA = [p.psum(f"psA{d}", [128, 128]) for d in range(2)]
    bpsA = [p.buf(f"psA{d}") for d in range(2)]
    pso = [p.psum(f"pso{d}", [128, 2, 128]) for d in range(2)]
    bpso = [p.buf(f"pso{d}") for d in range(2)]
    psU = [p.psum(f"psU{d}", [128, 256]) for d in range(2)]
    bpsU = [p.buf(f"psU{d}") for d in range(2)]
    bout = p.buf("out")
    cnt = {}

    def nxt(name, n):
        i = cnt.get(name, 0) % n
        cnt[name] = cnt.get(name, 0) + 1
        return i

    groups = [(0, NCC)] + [(c, GC) for c in range(NCC, NCH, GC)]
    order = {0: groups, 1: [groups[0]] + groups[1:][::-1]}
    kdv = kd.rearrange("d (c p) k -> d p c k", p=128)
    vtv = vt.rearrange("(c p) e -> p c e", p=128)

    def group(d, c0, ncg):
        ib = nxt(f"g{d}", NBUF)
        t0, n = c0 * 128, ncg * 128
        b_in = bin_[d][ib]
        p.dma(qg[d][ib][:, :n], qb[d, :, t0:t0 + n], w=[b_in], key=f"q{d}_{ib}")
        p.dma(kg[d][ib][:, :n], kb[d, :, t0:t0 + n], r=[], w=[], key=f"q{d}_{ib}")
        p.dma(kdg[d][ib][:, :ncg, :], kdv[d, :, c0:c0 + ncg, :], key=f"q{d}_{ib}")
        last_ld = p.dma(vg[d][ib][:, :ncg, :], vtv[:, c0:c0 + ncg, :], key=f"q{d}_{ib}")
        b_in.last_w = last_ld
        chunks = list(range(ncg)) if d == 0 else list(range(ncg))[::-1]
        for j in chunks:
            c = c0 + j
            cs = slice(j * 128, (j + 1) * 128)
            p.op("pe", lambda e, cs=cs: e.matmul(psA[d][:], lhsT=kg[d][ib][:, cs], rhs=qg[d][ib][:, cs], start=True, stop=True),
                 r=[b_in], w=[bpsA[d]])
            ia = nxt(f"Am{d}", 2)
            p.op("dve", lambda e, ia=ia: e.tensor_tensor(out=Am[d][ia][:], in0=psA[d][:], in1=mk[:, d, :], op=ALU.mult),
                 r=[bpsA[d], bmk], w=[bAm[d][ia]])
            for eh in range(2):
                p.op("pe", lambda e, eh=eh, cs=cs: e.matmul(pso[d][:, eh, :], lhsT=Sb[d][:, eh * 128:(eh + 1) * 128], rhs=qg[d][ib][:, cs],
                                                           start=True, stop=False), r=[bSb[d], b_in], w=[bpso[d]])
                p.op("pe", lambda e, eh=eh, j=j, ia=ia: e.matmul(pso[d][:, eh, :], lhsT=vg[d][ib][:, j, eh * 128:(eh + 1) * 128], rhs=Am[d][ia][:],
                                                                 start=False, stop=True), r=[b_in, bAm[d][ia]], w=[bpso[d]])
            p.op("act", lambda e, cs=cs: e.activation(out=og[d][ib][:, :, cs], in_=pso[d][:], func=AF.Copy), r=[bpso[d]], w=[bog[d][ib]])
            p.op("pe", lambda e, j=j: e.matmul(psU[d][:], lhsT=kdg[d][ib][:, j, :], rhs=vg[d][ib][:, j, :], start=True, stop=True),
                 r=[b_in], w=[bpsU[d]])
            p.op("dve", lambda e, c=c: e.scalar_tensor_tensor(out=S[d][:], in0=S[d][:], scalar=dc[:, d, c:c + 1], in1=psU[d][:],
                                                              op0=ALU.mult, op1=ALU.add), r=[bS[d], bdc, bpsU[d]], w=[bS[d]])
            p.op("act", lambda e: e.activation(out=Sb[d][:], in_=S[d][:], func=AF.Copy), r=[bS[d]], w=[bSb[d]])
        p.dma(oT[d, :, t0:t0 + n].rearrange("(e p) t -> p e t", p=128), og[d][ib][:, :, :n], r=[bog[d][ib]], w=[bout], key=f"og{d}_{ib}")

    for gi in range(len(groups)):
        for d in range(2):
            c0, ncg = order[d][gi]
            group(d, c0, ncg)
    p.wait_all("sp", [b for row in bog for b in row])
    p.emit()
    p.close()
    return nc


def build_gf3(T=4096, TC=256, NB=512, gla=True):
    nc = bass.Bass("TRN2", target_bir_lowering=False)
    xT = nc.dram_tensor("xT", [D, T], F32, kind="ExternalInput").ap()
    cT = nc.dram_tensor("cT", [D, TC], F32, kind="ExternalInput").ap()
    wo = nc.dram_tensor("wo", [D, D], BF16, kind="ExternalInput").ap()
    modl = nc.dram_tensor("modl", [128, 3, KC], F32, kind="ExternalInput").ap()
    modc = nc.dram_tensor("modc", [128, 3, KC], F32, kind="ExternalInput").ap()
    if gla:
        of = nc.dram_tensor("of", [2, D, T], F32, kind="ExternalInput").ap()
        ofc = nc.dram_tensor("ofc", [2, D, TC], F32, kind="ExternalInput").ap()
        sr = nc.dram_tensor("sr", [D, T], BF16, kind="ExternalInput").ap()
        src_ = nc.dram_tensor("src", [D, TC], BF16, kind="ExternalInput").ap()
        gh = nc.dram_tensor("gh", [128, 2], F32, kind="ExternalInput").ap()
    else:
        yT = nc.dram_tensor("yT", [D, T], BF16, kind="ExternalInput").ap()
        ycT = nc.dram_tensor("ycT", [D, TC], BF16, kind="ExternalInput").ap()
    oT = nc.dram_tensor("oT", [D, T], F32, kind="ExternalOutput").ap()
    ocT = nc.dram_tensor("ocT", [D, TC], F32, kind="ExternalOutput").ap()
    p = Prog(nc)
    cm = Common(p, NB)
    Wo = p.sbuf("Wo", [128, KC, D], BF16)
    bWo = p.buf("Wo")
    p.dma(Wo[:], wo.rearrange("(k p) d -> p k d", p=128), w=[bWo], key="Wo")
    ml, bml = cm.load_vec("modl", modl, [128, 3, KC])
    mc_, bmc = cm.load_vec("modc", modc, [128, 3, KC])
    if gla:
        ght, bgh = cm.load_vec("gh", gh, [128, 2])
    xin = [p.sbuf(f"xin{k}", [128, NB], F32) for k in range(KC)]
    bxin = [p.buf(f"xin{k}") for k in range(KC)]
    y = p.sbuf("y", [128, KC, NB], BF16)
    by = [p.buf(f"y{c}") for c in range(KC)]
    ob = [p.sbuf(f"ob{i}", [128, NB], F32) for i in range(2)]
    bob = [p.buf(f"ob{i}") for i in range(2)]
    NPS = 3
    ps = [p.psum(f"ps{i}", [128, NB]) for i in range(NPS)]
    bps = [p.buf(f"ps{i}") for i in range(NPS)]
    if gla:
        osum = [p.sbuf(f"osum{c}", [128, NB], F32) for c in range(KC)]
        bosum = [p.buf(f"osum{c}") for c in range(KC)]
        fa = [p.sbuf(f"fa{i}", [128, NB], F32) for i in range(2)]
        bfa = [p.buf(f"fa{i}") for i in range(2)]
        fb = [p.sbuf(f"fb{i}", [128, NB], F32) for i in range(2)]
        bfb = [p.buf(f"fb{i}") for i in range(2)]
        srb = [p.sbuf(f"srb{i}", [128, NB], BF16) for i in range(2)]
        bsrb = [p.buf(f"srb{i}") for i in range(2)]
        rstd = [p.sbuf(f"rstdh{i}", [128, NB], F32) for i in range(4)]
        brstd = [p.buf(f"rstdh{i}") for i in range(4)]
        pss = [p.psum(f"pssh{i}", [128, NB]) for i in range(4)]
        bpss = [p.buf(f"pssh{i}") for i in range(4)]
    bout = p.buf("out")
    cnt = {}

    def nxt(name, n):
        i = cnt.get(name, 0) % n
        cnt[name] = cnt.get(name, 0) + 1
        return i

    def block(src, dst, t0, n, mod, bmod, ofsrc, srsrc, ysrc):
        for k in range(KC):
            p.dma(xin[k][:, :n], src[k * 128:(k + 1) * 128, t0:t0 + n], w=[bxin[k]], key=f"xin{k}")
        if gla:
            for c in range(KC):
                ia, ib = nxt("fa", 2), nxt("fb", 2)
                p.dma(fa[ia][:, :n], ofsrc[0, c * 128:(c + 1) * 128, t0:t0 + n], w=[bfa[ia]], key=f"fa{ia}")
                p.dma(fb[ib][:, :n], ofsrc[1, c * 128:(c + 1) * 128, t0:t0 + n], w=[bfb[ib]], key=f"fb{ib}")
                p.op("dve", lambda e, c=c, ia=ia, ib=ib: e.tensor_tensor(out=osum[c][:, :n], in0=fa[ia][:, :n], in1=fb[ib][:, :n], op=ALU.add),
                     r=[bfa[ia], bfb[ib]], w=[bosum[c]])
                isq = cm.cnt % 2
                cm.cnt += 1
                p.op("act", lambda e, c=c, isq=isq: e.activation(out=cm.sq[isq][:, :n], in_=osum[c][:, :n], func=AF.Square),
                     r=[bosum[c]], w=[cm.b_sq[isq]])
                hh = c // 2
                p.op("pe", lambda e, c=c, isq=isq, hh=hh: e.matmul(pss[hh][:, :n], lhsT=cm.ones[:], rhs=cm.sq[isq][:, :n],
                                                                   start=(c % 2 == 0), stop=(c % 2 == 1)),
                     r=[cm.b_ones, cm.b_sq[isq]], w=[bpss[hh]])
                if c % 2 == 1:
                    p.op("act", lambda e, hh=hh: e.activation(out=rstd[hh][:, :n], in_=pss[hh][:, :n], func=AF.Sqrt,
                                                              bias=cm.eps[:, 0:1], scale=1.0 / 256), r=[bpss[hh], cm.b_eps], w=[brstd[hh]])
                    p.op("dve", lambda e, hh=hh: e.reciprocal(rstd[hh][:, :n], rstd[hh][:, :n]), r=[brstd[hh]], w=[brstd[hh]])
            for c in range(KC):
                hh, par = c // 2, c % 2
                js = nxt("srb", 2)
                p.dma(srb[js][:, :n], srsrc[c * 128:(c + 1) * 128, t0:t0 + n], w=[bsrb[js]], key=f"srb{js}")
                it = cm.cnt % 2
                cm.cnt += 1
                p.op("dve", lambda e, c=c, hh=hh, it=it: e.tensor_tensor(out=cm.tmp[it][:, :n], in0=osum[c][:, :n], in1=rstd[hh][:, :n], op=ALU.mult),
                     r=[bosum[c], brstd[hh]], w=[cm.b_tmp[it]])
                p.op("dve", lambda e, c=c, par=par, it=it, js=js: e.scalar_tensor_tensor(out=y[:, c, :n], in0=cm.tmp[it][:, :n],
                                                                                         scalar=ght[:, par:par + 1], in1=srb[js][:, :n],
                                                                                         op0=ALU.mult, op1=ALU.mult),
                     r=[cm.b_tmp[it], bgh, bsrb[js]], w=[by[c]])
        else:
            for c in range(KC):
                p.dma(y[:, c, :n], ysrc[c * 128:(c + 1) * 128, t0:t0 + n], w=[by[c]], key=f"y{c}")
        for co in range(KC):
            i = nxt("ps", NPS)
            for c in range(KC):
                p.op("pe", lambda e, i=i, c=c, co=co: e.matmul(ps[i][:, :n], lhsT=Wo[:, c, co * 128:(co + 1) * 128], rhs=y[:, c, :n],
                                                               start=(c == 0), stop=(c == KC - 1)), r=[bWo, by[c]], w=[bps[i]])
            j = nxt("ob", 2)
            p.op("dve", lambda e, i=i, j=j, co=co: e.scalar_tensor_tensor(out=ob[j][:, :n], in0=ps[i][:, :n], scalar=mod[:, 2, co:co + 1],
                                                                           in1=xin[co][:, :n], op0=ALU.mult, op1=ALU.add),
                 r=[bps[i], bmod, bxin[co]], w=[bob[j]])
            p.dma(dst[co * 128:(co + 1) * 128, t0:t0 + n], ob[j][:, :n], r=[bob[j]], w=[bout], key=f"ob{j}")

    if gla:
        block(cT, ocT, 0, TC, mc_, bmc, ofc, src_, None)
        for b in range(T // NB):
            block(xT, oT, b * NB, NB, ml, bml, of, sr, None)
    else:
        block(cT, ocT, 0, TC, mc_, bmc, None, None, ycT)
        for b in range(T // NB):
            block(xT, oT, b * NB, NB, ml, bml, None, None, yT)
    p.wait_all("sp", bob)
    p.emit()
    p.close()
    return nc


def build_prep(M, NCOL):
    CH = 2048
    nc = bass.Bass("TRN2", target_bir_lowering=False)
    wf = nc.dram_tensor("wf", [128, M], F32, kind="ExternalInput").ap()
    cin = nc.dram_tensor("cin", [128, KC, 4], F32, kind="ExternalInput").ap()
    wm = nc.dram_tensor("wm", [D, NCOL], F32, kind="ExternalInput").ap()
    bm = nc.dram_tensor("bm", [4, NCOL], F32, kind="ExternalInput").ap()
    wb = nc.dram_tensor("wb", [128, M], BF16, kind="ExternalOutput").ap()
    mod = nc.dram_tensor("mod", [4, NCOL], F32, kind="ExternalOutput").ap()
    p = Prog(nc)
    ci = p.sbuf("ci", [128, KC, 4], F32)
    bci = p.buf("ci")
    p.dma(ci[:], cin, w=[bci], key="ci")
    sc = p.sbuf("sc", [128, KC, 4], F32)
    bsc = p.buf("sc")
    p.op("act", lambda e: e.activation(out=sc[:], in_=ci[:], func=AF.Silu), r=[bci], w=[bsc])
    bmt = p.sbuf("bmt", [4, NCOL], F32)
    bbmt = p.buf("bmt")
    p.dma(bmt[:], bm, w=[bbmt], key="bmt")
    wmb = [p.sbuf(f"wmb{i}", [128, KC, 512], F32) for i in range(2)]
    bwmb = [p.buf(f"wmb{i}") for i in range(2)]
    psm = [p.psum(f"psm{i}", [4, 512]) for i in range(2)]
    bpsm = [p.buf(f"psm{i}") for i in range(2)]
    mo = [p.sbuf(f"mo{i}", [4, 512], F32) for i in range(2)]
    bmo = [p.buf(f"mo{i}") for i in range(2)]
    bout = p.buf("out")
    wmv = wm.rearrange("(k p) n -> p k n", p=128)
    for cb in range(NCOL // 512):
        i = cb % 2
        p.dma(wmb[i][:], wmv[:, :, cb * 512:(cb + 1) * 512], w=[bwmb[i]], key=f"wmb{i}")
        for k in range(KC):
            p.op("pe", lambda e, i=i, k=k: e.matmul(psm[i][:], lhsT=sc[:, k, :], rhs=wmb[i][:, k, :], start=(k == 0), stop=(k == KC - 1)),
                 r=[bsc, bwmb[i]], w=[bpsm[i]])
        p.op("dve", lambda e, i=i, cb=cb: e.tensor_tensor(out=mo[i][:], in0=psm[i][:], in1=bmt[:, cb * 512:(cb + 1) * 512], op=ALU.add),
             r=[bpsm[i], bbmt], w=[bmo[i]])
        p.dma(mod[:, cb * 512:(cb + 1) * 512], mo[i][:], r=[bmo[i]], w=[bout], key=f"mo{i}")
    NBF = 3
    fi = [p.sbuf(f"fi{i}", [128, CH], F32) for i in range(NBF)]
    bfi = [p.buf(f"fi{i}") for i in range(NBF)]
    fo = [p.sbuf(f"fo{i}", [128, CH], BF16) for i in range(NBF)]
    bfo = [p.buf(f"fo{i}") for i in range(NBF)]
    engs = ["dve", "act", "dve"]
    for ch in range(M // CH):
        i = ch % NBF
        p.dma(fi[i][:], wf[:, ch * CH:(ch + 1) * CH], w=[bfi[i]], key=f"fi{i}")
        en = engs[ch % 3]
        if en == "act":
            p.op("act", lambda e, i=i: e.activation(out=fo[i][:], in_=fi[i][:], func=AF.Copy), r=[bfi[i]], w=[bfo[i]])
        else:
            p.op(en, lambda e, i=i: e.tensor_copy(fo[i][:], fi[i][:]), r=[bfi[i]], w=[bfo[i]])
        p.dma(wb[:, ch * CH:(ch + 1) * CH], fo[i][:], r=[bfo[i]], w=[bout], key=f"fo{i}")
    p.wait_all("sp", bfo + bmo)
    p.emit()
    p.close()
    return nc


import ml_dtypes

NCORE = 8
BATCH = 2
SEQ = 16384
CTX = 256
SEGS = 4
TSEG = SEQ // SEGS
DEPTH = 4
_PROGS = {}
_DBG = None
_START = None


def _prog(name, fn):
    if name not in _PROGS:
        _PROGS[name] = fn()
    return _PROGS[name]


def _run(nc, in_maps):
    in_maps = [{k: np.ascontiguousarray(v) for k, v in m.items()} for m in in_maps]
    res = run_bass_kernel_spmd(nc, in_maps, core_ids=list(range(NCORE)))
    return res.results


def _lay(v):
    v = np.asarray(v, np.float32).reshape(-1, 8, 128)
    return np.ascontiguousarray(v.transpose(2, 0, 1))


def kernel(x, c, ctx, c_ctx, w_mod, b_mod, g_norm_mix, g_norm_mlp, w_mlp_in, w_mlp_out,
           gla_w_in, gla_w_g1, gla_w_g2, gla_b_g, gla_g_head, gla_w_out,
           conv_w_pw1, conv_b_pw1, conv_w_dw, conv_b_dw, conv_ln_g, conv_ln_b, conv_w_pw2, conv_b_pw2,
           attn_w_qkv, attn_g_q, attn_g_k, attn_w_out):
    f32 = np.float32
    A = lambda a: np.asarray(a, f32)
    x, c, ctx, c_ctx = A(x), A(c), A(ctx), A(c_ctx)
    wl = []
    for i in range(DEPTH):
        wl.append(("mi%d" % i, A(w_mlp_in[i])))
        wl.append(("mo%d" % i, A(w_mlp_out[i])))
    for j in range(2):
        wl.append(("gi%d" % j, A(gla_w_in[j])))
        wl.append(("go%d" % j, A(gla_w_out[j])))
        wl.append(("g1%d" % j, np.concatenate([A(gla_w_g1[j][0]), A(gla_w_g1[j][1])], 1)))
        wl.append(("g2%d" % j, np.ascontiguousarray(A(gla_w_g2[j]).transpose(1, 0, 2))))
    wl += [("cp1", A(conv_w_pw1[0])), ("cp2", A(conv_w_pw2[0])), ("aq", A(attn_w_qkv[0])), ("ao", A(attn_w_out[0]))]
    tot = sum(a.size for _, a in wl)
    unit = NCORE * 128 * 2048
    totp = (tot + unit - 1) // unit * unit
    flat = np.zeros(totp, f32)
    o = 0
    for _, a in wl:
        flat[o:o + a.size] = a.ravel()
        o += a.size
    M = totp // (NCORE * 128)
    flat = flat.reshape(NCORE, 128, M)
    wm_all = np.concatenate([A(w_mod[i]) for i in range(DEPTH)], 1)
    bm_all = np.concatenate([A(b_mod[i]) for i in range(DEPTH)], 0)
    NCOL = wm_all.shape[1] // NCORE
    rows = np.stack([c[0], c[1], c_ctx, np.zeros_like(c_ctx)])
    cin = np.ascontiguousarray(rows.reshape(4, 8, 128).transpose(2, 1, 0))
    nc = _prog("prep", lambda: build_prep(M, NCOL))
    maps = []
    for k in range(NCORE):
        maps.append(dict(wf=flat[k], cin=cin, wm=wm_all[:, k * NCOL:(k + 1) * NCOL],
                         bm=np.tile(bm_all[None, k * NCOL:(k + 1) * NCOL], (4, 1))))
    res = _run(nc, maps)
    wbf = np.concatenate([r["wb"].reshape(-1) for r in res])
    Wb = {}
    o = 0
    for nm, a in wl:
        Wb[nm] = wbf[o:o + a.size].reshape(a.shape)
        o += a.size
    mod_all = np.concatenate([r["mod"] for r in res], 1).reshape(4, DEPTH, 6, 1024)

    def mods(i, sub):
        sl = slice(3 * sub, 3 * sub + 3)
        return [_lay(mod_all[0, i, sl]), _lay(mod_all[1, i, sl])], _lay(mod_all[2, i, sl])

    xs = [np.ascontiguousarray(x[k // SEGS, (k % SEGS) * TSEG:(k % SEGS + 1) * TSEG].T) for k in range(NCORE)]
    cs = [np.ascontiguousarray(ctx[b].T) for b in range(BATCH)]

    def take(res, name="oT", cname="ocT"):
        return [r[name] for r in res], [res[b * SEGS][cname] for b in range(BATCH)]

    tp, tt = np.meshgrid(np.arange(128), np.arange(128), indexing="ij")
    tri = np.stack([(tp <= tt), (tp >= tt), (tp > tt), (tp < tt)], 1).astype(f32)
    msk = np.stack([(tp <= tt), (tp >= tt)], 1).astype(f32)

    if _START is not None:
        xs, cs = _START[1], _START[2]
    for i in range(_START[0] if _START is not None else 0, DEPTH):
        kind, j = i % 3, i // 3
        ml, mc = mods(i, 0)
        gmix = _lay(A(g_norm_mix[i])[None])
        if kind == 0:
            nc = _prog("gp1", lambda: build_gp1(TSEG, CTX))
            bgb = np.tile(A(gla_b_g[j])[None], (128, 1, 1))
            maps = [dict(xT=xs[k], cT=cs[k // SEGS], win=Wb["gi%d" % j], wg1=Wb["g1%d" % j], wg2=Wb["g2%d" % j], bgb=bgb,
                         gn=gmix, modl=ml[k // SEGS], modc=mc, tri=tri) for k in range(NCORE)]
            r1 = _run(nc, maps)
            nc = _prog("gs2", lambda: build_gs2(SEQ, CTX))
            maps = []
            for u in range(NCORE):
                b, hh = u // 4, u % 4
                rr = [r1[b * SEGS + s] for s in range(SEGS)]
                sl = slice(hh * 128, (hh + 1) * 128)
                sv = slice(hh * 256, (hh + 1) * 256)
                cat = lambda key, idx: np.concatenate([rr[0][key + "c"][idx]] + [q[key][idx] for q in rr], -1)
                maps.append(dict(
                    qb=np.stack([cat("qb", d * 4 + hh) for d in range(2)]),
                    kb=np.stack([cat("kb", d * 4 + hh) for d in range(2)]),
                    kd=np.stack([np.concatenate([rr[0]["kdc"][d][:, sl]] + [q["kd"][d][:, sl] for q in rr], 0) for d in range(2)]),
                    vt=np.concatenate([rr[0]["vtc"][:, sv]] + [q["vt"][:, sv] for q in rr], 0),
                    dec=np.stack([np.concatenate([rr[0]["decc"][:, d * 4 + hh]] + [q["dec"][:, d * 4 + hh] for q in rr], -1)
                                  for d in range(2)], 1),
                    msk=msk))
            r2 = _run(nc, maps)
            nc = _prog("gf3g", lambda: build_gf3(TSEG, CTX, gla=True))
            gh = np.ascontiguousarray(A(gla_g_head[j]).reshape(2, 128).T)
            maps = []
            for k in range(NCORE):
                b, s = k // SEGS, k % SEGS
                ou = [r2[b * 4 + hh]["oT"] for hh in range(4)]
                of = np.concatenate([o_[:, :, CTX + s * TSEG:CTX + (s + 1) * TSEG] for o_ in ou], 1)
                ofc = np.concatenate([o_[:, :, :CTX] for o_ in ou], 1)
                maps.append(dict(xT=xs[k], cT=cs[b], wo=Wb["go%d" % j], modl=ml[b], modc=mc, of=of, ofc=ofc,
                                 sr=r1[k]["sr"], src=r1[b * SEGS]["src"], gh=gh))
            xs, cs = take(_run(nc, maps))
            if _DBG is not None:
                _DBG.append(("gla_raw", r1, r2))
        elif kind == 1:
            nc = _prog("conv", lambda: build_conv(TSEG, CTX))
            vecs = _lay(np.stack([A(g_norm_mix[i]), A(conv_b_pw1[j])[:1024], A(conv_b_pw1[j])[1024:], A(conv_b_dw[j]),
                                  A(conv_ln_g[j]), A(conv_ln_b[j]), A(conv_b_pw2[j])]))
            wdw = np.ascontiguousarray(A(conv_w_dw[j]).reshape(31, 8, 128).transpose(2, 1, 0))
            maps = []
            for k in range(NCORE):
                b, s = k // SEGS, k % SEGS
                xe = np.zeros((1024, TSEG + 30), f32)
                xe[:, 15:15 + TSEG] = xs[k]
                if s > 0:
                    xe[:, :15] = xs[k - 1][:, -15:]
                if s < SEGS - 1:
                    xe[:, -15:] = xs[k + 1][:, :15]
                ce = np.zeros((1024, CTX + 30), f32)
                ce[:, 15:15 + CTX] = cs[b]
                hm = np.tile(np.array([[1.0 if s > 0 else 0.0, 1.0 if s < SEGS - 1 else 0.0]], f32), (128, 1))
                maps.append(dict(xe=xe, ce=ce, wp1=Wb["cp1"], wp2=Wb["cp2"], ident=np.eye(128, dtype=f32), vecs=vecs, wdw=wdw,
                                 modl=ml[b], modc=mc, hmask=hm))
            xs, cs = take(_run(nc, maps))
        else:
            nc = _prog("at1", lambda: build_at1(TSEG, CTX))
            gq, gk = A(attn_g_q[j]), A(attn_g_k[j])
            inv = (10000.0 ** (-np.arange(16, dtype=f32) / 16)).astype(f32)
            bdo = np.kron(np.eye(2), np.ones((64, 64))).astype(f32)
            maps = []
            for k in range(NCORE):
                b, s = k // SEGS, k % SEGS
                pos = np.arange(TSEG) + s * TSEG
                ang = np.concatenate([(pos // 64).astype(f32)[:, None] * inv, (pos % 64).astype(f32)[:, None] * inv], -1)
                cos, sin = np.cos(ang).astype(f32), np.sin(ang).astype(f32)
                maps.append(dict(xT=xs[k], cT=cs[b], wqkv=Wb["aq"], gn=gmix, modl=ml[b], modc=mc,
                                 gqk=np.stack([np.tile(gq, 2), np.tile(gk, 2)], 1),
                                 gqkb=np.tile(np.concatenate([gq, gk])[None], (128, 1)),
                                 cosT=np.tile(cos.T, (4, 1)), sinT=np.concatenate([sin.T, -sin.T, sin.T, -sin.T]), bdo=bdo))
            r1 = _run(nc, maps)
            nc = _prog("at2", lambda: build_at2(SEQ, CTX))
            maps = []
            ones64 = np.ones((CTX + SEQ, 64), ml_dtypes.bfloat16)
            for u in range(NCORE):
                b, kv = u // 4, u % 4
                rr = [r1[b * SEGS + s] for s in range(SEGS)]
                rq = slice(kv * 256, (kv + 1) * 256)
                rk = slice(kv * 64, (kv + 1) * 64)
                vcat = np.concatenate([rr[0]["vc"][:, rk]] + [q["v"][:, rk] for q in rr], 0)
                maps.append(dict(q4=np.concatenate([q["qT"][rq] for q in rr], 1), qc4=rr[0]["qcT"][rq],
                                 kT=np.concatenate([rr[0]["kcT"][rk]] + [q["kT"][rk] for q in rr], 1),
                                 vx=np.concatenate([vcat, ones64], 1), negm=rr[0]["negm"]))
            r2 = _run(nc, maps)
            nc = _prog("gf3a", lambda: build_gf3(TSEG, CTX, gla=False))
            maps = []
            for k in range(NCORE):
                b, s = k // SEGS, k % SEGS
                yT = np.concatenate([r2[b * 4 + kv]["o4"][:, s * TSEG:(s + 1) * TSEG] for kv in range(4)], 0)
                ycT = np.concatenate([r2[b * 4 + kv]["oc4"] for kv in range(4)], 0)
                maps.append(dict(xT=xs[k], cT=cs[b], wo=Wb["ao"], modl=ml[b], modc=mc, yT=yT, ycT=ycT))
            xs, cs = take(_run(nc, maps))
        if _DBG is not None:
            _DBG.append(("mix%d" % i, [a.copy() for a in xs], [a.copy() for a in cs]))
        ml2, mc2 = mods(i, 1)
        nc = _prog("mlp", lambda: build_mlp(TSEG, CTX))
        gm = _lay(A(g_norm_mlp[i])[None])
        maps = [dict(xT=xs[k], cT=cs[k // SEGS], w1=Wb["mi%d" % i], w2=Wb["mo%d" % i], gn=gm, modl=ml2[k // SEGS], modc=mc2)
                for k in range(NCORE)]
        xs, cs = take(_run(nc, maps))
        if _DBG is not None:
            _DBG.append(("mlp%d" % i, [a.copy() for a in xs], [a.copy() for a in cs]))
    out = np.empty((BATCH, SEQ, 1024), f32)
    for k in range(NCORE):
        out[k // SEGS, (k % SEGS) * TSEG:(k % SEGS + 1) * TSEG] = xs[k].T
    return out
```
